# Optimizing a Trainium2 kernel written in Bass

```python
import jax, jax.numpy as jnp
from jax import lax
import numpy as np

D_MODEL = 2048
BATCH = 8
SEQ = 2048
DEPTH = 1

GRID_W = 64
CTX_LEN = 256
N_HEADS = 16
HEAD_DIM = D_MODEL // N_HEADS
ATTN_WIDTH = N_HEADS * HEAD_DIM
F_GROUPS = 4
F_GROUP_DIM = D_MODEL // 8
F_WIDTH = F_GROUPS * F_GROUP_DIM
MAX_WIN_R = 8
WIN_C = 16
ROT_PER_AXIS = HEAD_DIM // 2
ROPE_BASE = 10000.0
EPS = 1e-6

OFF_ZF = F_WIDTH
OFF_Q = 2 * F_WIDTH
OFF_K = OFF_Q + ATTN_WIDTH
OFF_V = OFF_K + ATTN_WIDTH
OFF_ZA = OFF_V + ATTN_WIDTH
OFF_GF = OFF_ZA + ATTN_WIDTH
OFF_GA = OFF_GF + D_MODEL
IN_WIDTH = OFF_GA + D_MODEL
SPLIT_POINTS = (OFF_ZF, OFF_Q, OFF_K, OFF_V, OFF_ZA, OFF_GF, OFF_GA)

kernel_name = "hybrid_fourier_natten_dit_block"


def _rms(x):
    xf = x.astype(jnp.float32)
    return (xf * lax.rsqrt(jnp.mean(xf * xf, axis=-1, keepdims=True) + EPS)).astype(x.dtype)


def _qk_norm(x, gain):
    return _rms(x) * gain.astype(x.dtype)


def _heads(t):
    B, N, _ = t.shape
    return t.reshape(B, N, N_HEADS, HEAD_DIM).transpose(0, 2, 1, 3)


def _merge_heads(t):
    B, H, N, Dh = t.shape
    return t.transpose(0, 2, 1, 3).reshape(B, N, H * Dh)


def _axial_rope(n_tok):
    t = jnp.arange(n_tok)
    pos = jnp.stack([t // GRID_W, t % GRID_W], axis=-1).astype(jnp.float32)
    n_freq = ROT_PER_AXIS // 2
    inv_freq = ROPE_BASE ** (-jnp.arange(n_freq, dtype=jnp.float32) / n_freq)
    ang = pos[..., None] * inv_freq
    return jnp.cos(ang), jnp.sin(ang)


def _apply_rope(x, cos, sin):
    B, H, N, Dh = x.shape
    xr = x.reshape(B, H, N, 2, 2, ROT_PER_AXIS // 2)
    x0, x1 = xr[..., 0, :], xr[..., 1, :]
    cos = cos.astype(x.dtype)
    sin = sin.astype(x.dtype)
    out = jnp.stack([x0 * cos - x1 * sin, x1 * cos + x0 * sin], axis=-2)
    return out.reshape(B, H, N, Dh)


def _fourier_mix(u):
    B, N, _ = u.shape
    ug = u.reshape(B, N, F_GROUPS, F_GROUP_DIM).astype(jnp.float32)
    y = jnp.fft.fft2(ug, axes=(1, 3), norm="ortho").real
    return y.reshape(B, N, F_WIDTH).astype(u.dtype)


def _merge_branches(u_f, z_f, o, z_a, g_f, g_a, w_f_out, w_a_out, w_out):
    y_f = (_fourier_mix(u_f) * jax.nn.silu(z_f)) @ w_f_out
    y_a = (o * jax.nn.silu(z_a)) @ w_a_out
    y = jax.nn.sigmoid(g_f) * y_f + jax.nn.sigmoid(g_a) * y_a
    return y @ w_out


def _neighbourhood_attention(q, k, v, k_ctx, v_ctx, rpb):
    B, H, S, Dh = q.shape
    rows = S // GRID_W
    win_r = min(MAX_WIN_R, rows)
    scale = HEAD_DIM ** -0.5
    qg = q.reshape(B, H, rows, GRID_W, Dh)
    kg = k.reshape(B, H, rows, GRID_W, Dh)
    vg = v.reshape(B, H, rows, GRID_W, Dh)
    cols = jnp.arange(GRID_W)
    col_start = jnp.clip(cols - WIN_C // 2, 0, GRID_W - WIN_C)
    col_idx = col_start[:, None] + jnp.arange(WIN_C)[None, :]
    dc_idx = col_idx - cols[:, None] + (WIN_C - 1)

    def one_row(r):
        r_start = jnp.clip(r - win_r // 2, 0, rows - win_r)
        q_r = lax.dynamic_index_in_dim(qg, r, axis=2, keepdims=False)
        k_band = lax.dynamic_slice_in_dim(kg, r_start, win_r, axis=2)
        v_band = lax.dynamic_slice_in_dim(vg, r_start, win_r, axis=2)
        k_nb = k_band[:, :, :, col_idx, :]
        v_nb = v_band[:, :, :, col_idx, :]
        dr_idx = r_start + jnp.arange(win_r) - r + (MAX_WIN_R - 1)
        bias = rpb[:, dr_idx[:, None, None], dc_idx[None, :, :]]
        bias = bias.transpose(0, 2, 1, 3).reshape(H, GRID_W, win_r * WIN_C).astype(jnp.float32)
        s_loc = jnp.einsum('bhqd,bhiqjd->bhqij', q_r, k_nb).reshape(B, H, GRID_W, win_r * WIN_C)
        s_ctx = jnp.einsum('bhqd,bhld->bhql', q_r, k_ctx)
        logits = jnp.concatenate([s_loc.astype(jnp.float32) * scale + bias,
                                  s_ctx.astype(jnp.float32) * scale], axis=-1)
        p = jax.nn.softmax(logits, axis=-1).astype(v.dtype)
        n_loc = win_r * WIN_C
        p_loc = p[..., :n_loc].reshape(B, H, GRID_W, win_r, WIN_C)
        p_ctx = p[..., n_loc:]
        return (jnp.einsum('bhqij,bhiqjd->bhqd', p_loc, v_nb)
                + jnp.einsum('bhql,bhld->bhqd', p_ctx, v_ctx))

    out = lax.map(one_row, jnp.arange(rows))
    return out.transpose(1, 2, 0, 3, 4).reshape(B, H, S, Dh)


def _latent_mixer(h, k_ctx, v_ctx, w_in, q_gain, k_gain, rpb, w_f_out, w_a_out, w_out):
    S = h.shape[1]
    u_f, z_f, q, k, v, z_a, g_f, g_a = jnp.split(h @ w_in, SPLIT_POINTS, axis=-1)
    cos, sin = _axial_rope(S)
    q = _apply_rope(_qk_norm(_heads(q), q_gain), cos, sin)
    k = _apply_rope(_qk_norm(_heads(k), k_gain), cos, sin)
    o = _merge_heads(_neighbourhood_attention(q, k, _heads(v), k_ctx, v_ctx, rpb))
    return _merge_branches(u_f, z_f, o, z_a, g_f, g_a, w_f_out, w_a_out, w_out)


def _context_kv(h_ctx, w_in, k_gain):
    k, v = jnp.split(h_ctx @ w_in[:, OFF_K:OFF_ZA], [ATTN_WIDTH], axis=-1)
    return _qk_norm(_heads(k), k_gain), _heads(v)


def _context_mixer(h_ctx, w_in, q_gain, k_gain, w_f_out, w_a_out, w_out):
    u_f, z_f, q, k, v, z_a, g_f, g_a = jnp.split(h_ctx @ w_in, SPLIT_POINTS, axis=-1)
    q = _qk_norm(_heads(q), q_gain)
    k = _qk_norm(_heads(k), k_gain)
    v = _heads(v)
    logits = jnp.einsum('bhqd,bhkd->bhqk', q, k).astype(jnp.float32) * (HEAD_DIM ** -0.5)
    p = jax.nn.softmax(logits, axis=-1).astype(v.dtype)
    o = _merge_heads(jnp.einsum('bhqk,bhkd->bhqd', p, v))
    return _merge_branches(u_f, z_f, o, z_a, g_f, g_a, w_f_out, w_a_out, w_out), k, v


def setup_inputs(seed: int = 0) -> dict:
    key = jax.random.key(seed)
    ks = jax.random.split(key, 14)
    f32 = jnp.float32
    nrm = lambda k, shape, s: jax.random.normal(k, shape, f32) * s
    return {
        "x": nrm(ks[0], (BATCH, SEQ, D_MODEL), 1.0),
        "c": nrm(ks[1], (BATCH, D_MODEL), 1.0),
        "ctx": nrm(ks[2], (BATCH, CTX_LEN, D_MODEL), 1.0),
        "c_ctx": nrm(ks[3], (D_MODEL,), 1.0),
        "w_mod": nrm(ks[4], (DEPTH, D_MODEL, 3 * D_MODEL), D_MODEL ** -0.5),
        "b_mod": nrm(ks[5], (DEPTH, 3 * D_MODEL), 0.02),
        "w_in": nrm(ks[6], (DEPTH, D_MODEL, IN_WIDTH), D_MODEL ** -0.5),
        "q_gain": 1.0 + nrm(ks[7], (DEPTH, HEAD_DIM), 0.02),
        "k_gain": 1.0 + nrm(ks[8], (DEPTH, HEAD_DIM), 0.02),
        "rpb": nrm(ks[9], (DEPTH, N_HEADS, 2 * MAX_WIN_R - 1, 2 * WIN_C - 1), 0.1),
        "w_f_out": nrm(ks[10], (DEPTH, F_WIDTH, D_MODEL), F_WIDTH ** -0.5),
        "w_a_out": nrm(ks[11], (DEPTH, ATTN_WIDTH, D_MODEL), ATTN_WIDTH ** -0.5),
        "w_out": nrm(ks[12], (DEPTH, D_MODEL, D_MODEL), D_MODEL ** -0.5),
    }


def reference(x, c, ctx, c_ctx, w_mod, b_mod, w_in, q_gain, k_gain, rpb, w_f_out, w_a_out, w_out):
    silu_c = jax.nn.silu(c)
    silu_cc = jax.nn.silu(c_ctx)
    for l in range(DEPTH):
        mod_x = silu_c @ w_mod[l] + b_mod[l]
        mod_c = silu_cc @ w_mod[l] + b_mod[l]
        shift_x, scale_x, gate_x = jnp.split(mod_x[:, None, :], 3, axis=-1)
        shift_c, scale_c, gate_c = jnp.split(mod_c, 3, axis=-1)
        h_ctx = _rms(ctx) * (1.0 + scale_c) + shift_c
        h_x = _rms(x) * (1.0 + scale_x) + shift_x
        if l < DEPTH - 1:
            y_ctx, k_ctx, v_ctx = _context_mixer(h_ctx, w_in[l], q_gain[l], k_gain[l],
                                                 w_f_out[l], w_a_out[l], w_out[l])
            ctx_next = ctx + gate_c * y_ctx
        else:
            k_ctx, v_ctx = _context_kv(h_ctx, w_in[l], k_gain[l])
            ctx_next = ctx
        y_x = _latent_mixer(h_x, k_ctx, v_ctx, w_in[l], q_gain[l], k_gain[l], rpb[l],
                            w_f_out[l], w_a_out[l], w_out[l])
        x = x + gate_x * y_x
        ctx = ctx_next
    return x
```

```python
import numpy as np
import ml_dtypes
from contextlib import ExitStack
import concourse.bass as bass
import concourse.mybir as mybir
from concourse.bass_utils import run_bass_kernel_spmd

F32 = mybir.dt.float32
BF16 = mybir.dt.bfloat16
AF = mybir.ActivationFunctionType
ALU = mybir.AluOpType
AX = mybir.AxisListType

D = 2048
SEQ = 2048
CTX = 256
NT = SEQ + CTX
H = 16
DH = 128
FW = 1024
OFF_ZF, OFF_Q, OFF_K, OFF_V, OFF_ZA, OFF_GF, OFF_GA = 1024, 2048, 4096, 6144, 8192, 10240, 12288
INW = 14336
EPS = 1e-6
NEG = -30000.0
_LA = 4
_DEFER = True


class Sched:
    def __init__(self, nc, stack):
        self.nc = nc
        self.stack = stack
        self.eng = {"pe": nc.tensor, "act": nc.scalar, "dve": nc.vector,
                    "pool": nc.gpsimd, "sp": nc.sync}
        self.esem = {}
        self.ecnt = {}
        for e in ("pe", "act", "dve", "pool"):
            self.esem[e] = stack.enter_context(nc.semaphore("es_" + e))
            self.ecnt[e] = 0
        self.waited = {e: {} for e in self.eng}
        self.lastw = {}
        self.readers = {}
        self.dsem = {}
        self.bank_last = {}

    @staticmethod
    def _bank(name):
        if len(name) >= 3 and name[:2] in ("pa", "pb", "pt") and name[2].isdigit():
            return name[:3]
        return None

    def _collect(self, reads, writes, eng=None):
        deps = []
        for n in tuple(reads) + tuple(writes):
            b = self._bank(n)
            if b is not None:
                for e2, ev in self.bank_last.get(b, {}).items():
                    if e2 != eng:
                        deps.append(ev)
        for r in reads:
            ev = self.lastw.get(r)
            if ev is not None:
                deps.append(ev)
        for w in writes:
            ev = self.lastw.get(w)
            if ev is not None:
                deps.append(ev)
            deps.extend(self.readers.get(w, ()))
        return deps

    def _emit_waits(self, eng, deps):
        e = self.eng[eng]
        best = {}
        for (key, sem, val, src) in deps:
            if eng == "pe" and src == "pe":
                continue
            if self.waited[eng].get(key, 0) >= val:
                continue
            if best.get(key, (None, 0))[1] < val:
                best[key] = (sem, val)
        for key, (sem, val) in best.items():
            e.wait_ge(sem, val)
            self.waited[eng][key] = val

    def _record(self, ev, reads, writes, eng=None):
        for n in tuple(reads) + tuple(writes):
            b = self._bank(n)
            if b is not None:
                self.bank_last.setdefault(b, {})[eng] = ev
        for w in writes:
            self.lastw[w] = ev
            self.readers[w] = []
        for r in reads:
            self.readers.setdefault(r, []).append(ev)

    def op(self, eng, fn, reads=(), writes=()):
        deps = self._collect(reads, writes, eng)
        self._emit_waits(eng, deps)
        ins = fn(self.eng[eng])
        self.ecnt[eng] += 1
        ins.then_inc(self.esem[eng], 1)
        ev = ("e_" + eng, self.esem[eng], self.ecnt[eng], eng)
        self._record(ev, reads, writes, eng)
        return ev

    def dma(self, queue, slot, fn, reads=(), writes=()):
        if slot not in self.dsem:
            self.dsem[slot] = [self.stack.enter_context(self.nc.semaphore("ds_" + slot)), 0]
        deps = self._collect(reads, writes)
        st = self.dsem[slot]
        if st[1] > 0:
            deps.append(("d_" + slot, st[0], st[1], "dma"))
        self._emit_waits(queue, deps)
        ins = fn(self.eng[queue])
        st[1] += 16
        ins.then_inc(st[0], 16)
        ev = ("d_" + slot, st[0], st[1], "dma")
        self._record(ev, reads, writes)
        return ev

    def _all_events(self):
        evs = []
        for e in self.esem:
            if self.ecnt[e] > 0:
                evs.append(("e_" + e, self.esem[e], self.ecnt[e], "x"))
        for slot, st in self.dsem.items():
            if st[1] > 0:
                evs.append(("d_" + slot, st[0], st[1], "dma"))
        return evs

    def barrier(self):
        evs = self._all_events()
        for eng in self.eng:
            self._emit_waits(eng, evs)
        self.lastw = {}
        self.readers = {}
        self.bank_last = {}

    def finish(self, eng="sp"):
        self._emit_waits(eng, self._all_events())


def mm(S, out, pairs, reads, writes):
    def fn(e):
        n = len(pairs)
        ins = None
        for i, (l, r) in enumerate(pairs):
            ins = e.matmul(out, l, r, start=(i == 0), stop=(i == n - 1))
        return ins
    return S.op("pe", fn, reads, writes)


def _rs(r):
    return min(max(r - 4, 0), 24)


def attn_geometry():
    blocks = []
    lo, hi = 10 ** 9, -10 ** 9
    for a in range(8):
        r0, r1 = 4 * a, 4 * a + 3
        j0 = _rs(r0) // 2
        j1 = (_rs(r1) + 7) // 2
        js = list(range(j0, j1 + 1))
        blocks.append(js)
        for j in js:
            lo = min(lo, r0 - 2 * j)
            hi = max(hi, r1 - 2 * j)
    ioff = -lo
    ni = hi - lo + 1
    return blocks, ioff, ni


ATT_BLOCKS, IOFF, NI = attn_geometry()


def host_bias_tables(rpb):
    rpb = np.asarray(rpb, np.float32)
    out = np.full((H, 128, 2, NI, 64), NEG, np.float32)
    c = np.arange(64)
    cs = np.clip(c - 8, 0, 48)
    kc = np.arange(64)
    colvalid = (kc[:, None] >= cs[None, :]) & (kc[:, None] <= cs[None, :] + 15)
    dc = kc[:, None] - c[None, :] + 15
    dcc = np.clip(dc, 0, 30)
    for hf in range(2):
        for i in range(NI):
            delta = hf - (i - IOFF)
            if abs(delta) > 7:
                continue
            vals = rpb[:, delta + 7, :][:, dcc]
            vals = np.where(colvalid[None], vals, NEG)
            out[:, hf * 64:(hf + 1) * 64, 1, i, :] = vals
            if -4 <= delta <= 3:
                out[:, hf * 64:(hf + 1) * 64, 0, i, :] = vals
    return out


def host_consts():
    n = np.arange(2048, dtype=np.float64)
    ang = 2 * np.pi * ((n[:, None] * n[None, :]) % 2048) / 2048.0
    cn = (np.cos(ang) / np.sqrt(2048.0)).astype(ml_dtypes.bfloat16)
    sn = (np.sin(ang) / np.sqrt(2048.0)).astype(ml_dtypes.bfloat16)
    m = np.arange(256, dtype=np.float64)
    angc = 2 * np.pi * ((m[:, None] * m[None, :]) % 256) / 256.0
    cc = (np.cos(angc) / 16.0).astype(ml_dtypes.bfloat16)
    nsc = (-np.sin(angc) / 16.0).astype(ml_dtypes.bfloat16)
    t = np.arange(2048)
    pos = np.stack([t // 64, t % 64], -1).astype(np.float32)
    inv = (10000.0 ** (-np.arange(32, dtype=np.float32) / 32)).astype(np.float32)
    a = pos[..., None] * inv
    rc = np.ones((NT, 64), np.float32)
    rs_ = np.zeros((NT, 64), np.float32)
    rc[:2048] = np.cos(a).reshape(2048, 64)
    rs_[:2048] = np.sin(a).reshape(2048, 64)
    return dict(cn=cn, sn=sn, cc=cc, nsc=nsc, ropec=rc, ropes=rs_,
                ident=np.eye(128).astype(ml_dtypes.bfloat16),
                ident32=np.eye(128).astype(np.float32))


def build_nc(debug=False, stop_after=None):
    nc = bass.Bass("TRN2", target_bir_lowering=False)
    dt_in = lambda n, s, d=F32: nc.dram_tensor(n, s, d, kind="ExternalInput").ap()
    x = dt_in("x", [SEQ, D])
    ctx = dt_in("ctx", [CTX, D])
    cvec = dt_in("cvec", [2, D])
    w_mod = dt_in("w_mod", [D, 3 * D])
    b_mod = dt_in("b_mod", [1, 3 * D])
    w_in = dt_in("w_in", [D, INW])
    qk_gain = dt_in("qk_gain", [1, 256])
    btab = dt_in("btab", [H, 128, 2 * NI * 64])
    w_f_out = dt_in("w_f_out", [FW, D])
    w_a_out = dt_in("w_a_out", [D, D])
    w_out = dt_in("w_out", [D, D])
    cn = dt_in("cn", [2048, 2048], BF16)
    sn = dt_in("sn", [2048, 2048], BF16)
    cc = dt_in("cc", [256, 256], BF16)
    nsc = dt_in("nsc", [256, 256], BF16)
    ropec = dt_in("ropec", [NT, 64])
    ropes = dt_in("ropes", [NT, 64])
    ident = dt_in("ident", [128, 128], BF16)
    ident32 = dt_in("ident32", [128, 128])
    out = nc.dram_tensor("out", [SEQ, D], F32, kind="ExternalOutput").ap()
    skind = dict(kind="ExternalOutput") if debug else {}
    fzt = nc.dram_tensor("fzt", [FW, SEQ], BF16, **skind).ap()
    ozt = nc.dram_tensor("ozt", [D, SEQ], BF16, **skind).ap()
    yt = nc.dram_tensor("yt", [D, SEQ], BF16, **skind).ap()
    gscr = nc.dram_tensor("gscr", [1, D], F32, **skind).ap()
    if debug:
        dbg_ht = nc.dram_tensor("dbg_ht", [128, 16 * NT], BF16, kind="ExternalOutput").ap()

    w_in_v = w_in.rearrange("(kc p) n -> p kc n", p=128)

    with ExitStack() as top:
        S = Sched(nc, top)
        T = lambda st, n, s, d: st.enter_context(nc.sbuf_tensor(n, s, d))
        pa = [top.enter_context(nc.psum_tensor("pa%d" % i, [128, 512], F32)) for i in range(2)]
        ptt = [top.enter_context(nc.psum_tensor("pt%d" % i, [128, 1024], BF16)) for i in range(2)]
        pt = [p[:] for p in ptt]
        pb = [top.enter_context(nc.psum_tensor("pb%d" % i, [128, 512], F32)) for i in range(4)]
        idb = T(top, "idb", [128, 128], BF16)
        id32 = T(top, "id32", [128, 128], F32)
        ones32 = T(top, "ones32", [128, 128], F32)
        onesb = T(top, "onesb", [128, 128], BF16)
        epst = T(top, "epst", [128, 1], F32)
        S.dma("sp", "c0", lambda e: e.dma_start(out=idb[:], in_=ident), writes=["idb"])
        S.dma("sp", "c1", lambda e: e.dma_start(out=id32[:], in_=ident32), writes=["id32"])
        S.op("dve", lambda e: e.memset(ones32[:], 1.0), writes=["ones32"])
        S.op("dve", lambda e: e.memset(onesb[:], 1.0), writes=["onesb"])
        S.op("dve", lambda e: e.memset(epst[:], EPS), writes=["epst"])
        mid = ExitStack()
        hT = T(mid, "hT", [128, 16, NT], BF16)
        WA = T(mid, "WA", [128, 16, 512], BF16)
        WB = T(mid, "WB", [128, 16, 512], BF16)

        evac_i = [0]

        def evac(out_ap, in_ap, reads, writes, func=None):
            evac_i[0] += 1
            if func is not None or evac_i[0] % 2 == 0:
                f = func if func is not None else AF.Copy
                S.op("act", lambda e: e.activation(out=out_ap, in_=in_ap, func=f), reads, writes)
            else:
                S.op("dve", lambda e: e.tensor_copy(out=out_ap, in_=in_ap), reads, writes)

        def load_w(buf, bufname, col0, ncols=512, queue="pool"):
            S.dma(queue, "w_" + bufname,
                  lambda e: e.dma_start(out=buf[:, :, 0:ncols], in_=w_in_v[:, :, col0:col0 + ncols]),
                  writes=[bufname])

        with ExitStack() as ph:
            cv = T(ph, "cv", [33, D], F32)
            cT = T(ph, "cT", [128, 16, 33], F32)
            bmc = [T(ph, "bmc%d" % i, [1, 512], F32) for i in range(2)]
            mrow = [T(ph, "mrow%d" % i, [33, 512], F32) for i in range(2)]
            wm = [T(ph, "wm%d" % i, [128, 8, 512], F32) for i in range(2)]
            wm = [w[:] for w in wm] + [w[:].rearrange("p k n -> p (k n)").bitcast(F32).rearrange("p (k n) -> p k n", n=512) for w in (WA, WB)]
            wmn = ["wm0", "wm1", "WA", "WB"]
            modT = T(ph, "modT", [128, 2, 16, 33], F32)
            xt = [T(ph, "xt%d" % i, [128, D], F32) for i in range(2)]
            sq = T(ph, "sq", [128, D], BF16)
            xs = [T(ph, "xs%d" % i, [128, D], BF16) for i in range(2)]
            st = [T(ph, "st%d" % i, [128, 4], F32) for i in range(2)]
            w_mod_v = w_mod.rearrange("(kc p) n -> p kc n", p=128)
            S.op("dve", lambda e: e.memset(cv[:], 0.0), writes=["cv"])
            S.dma("sp", "c0", lambda e: e.dma_start(out=cv[0:1, :], in_=cvec[0:1, :]), writes=["cv"])
            S.dma("sp", "c1", lambda e: e.dma_start(out=cv[32:33, :], in_=cvec[1:2, :]), writes=["cv"])

            def load_wm0(cb):
                for kh in range(2):
                    sl = (2 * cb + kh) % 4
                    S.dma("pool", "wmd%d" % sl,
                          lambda e, sl=sl, kh=kh: e.dma_start(out=wm[sl], in_=w_mod_v[:, kh * 8:(kh + 1) * 8, cb * 512:(cb + 1) * 512]),
                          writes=[wmn[sl]])
                S.dma("sp", "bmc%d" % (cb % 2),
                      lambda e: e.dma_start(out=bmc[cb % 2][:], in_=b_mod[0:1, cb * 512:(cb + 1) * 512]),
                      writes=["bmc%d" % (cb % 2)])
            load_wm0(0)
            load_wm0(1)
            S.op("act", lambda e: e.activation(out=cv[:], in_=cv[:], func=AF.Silu), reads=["cv"], writes=["cv"])
            for half in range(2):
                pv = pa[half][:, 0:264].rearrange("p (k m) -> p k m", m=33)

                def fn(e, half=half, pv=pv):
                    ins = None
                    for k in range(8):
                        kc = half * 8 + k
                        ins = e.transpose(pv[:, k, :], cv[0:33, kc * 128:(kc + 1) * 128], id32[0:33, 0:33])
                    return ins
                S.op("pe", fn, reads=["cv", "id32"], writes=["pa%d" % half])
                S.op("dve", lambda e, half=half, pv=pv: e.tensor_copy(out=cT[:, half * 8:half * 8 + 8, :], in_=pv),
                     reads=["pa%d" % half], writes=["cT%d" % half])

            def x_tile(t):
                s = t % 2
                src_ap = x[t * 128:(t + 1) * 128, :] if t < 16 else ctx[(t - 16) * 128:(t - 15) * 128, :]
                S.dma("sp", "xt%d" % s, lambda e: e.dma_start(out=xt[s][:], in_=src_ap), writes=["xt%d" % s])
                S.op("act", lambda e: e.activation(out=sq[:], in_=xt[s][:], func=AF.Square, accum_out=st[s][:, 0:1]),
                     reads=["xt%d" % s], writes=["sq", "st%d" % s])
                S.op("act", lambda e: e.activation(out=st[s][:, 1:2], in_=st[s][:, 0:1], func=AF.Ln, scale=1.0 / D, bias=epst[:]),
                     reads=["st%d" % s, "epst"], writes=["st%d" % s])
                S.op("act", lambda e: e.activation(out=st[s][:, 2:3], in_=st[s][:, 1:2], func=AF.Exp, scale=-0.5),
                     reads=["st%d" % s], writes=["st%d" % s])
                S.op("dve", lambda e: e.tensor_scalar(out=xs[s][:], in0=xt[s][:], scalar1=st[s][:, 2:3], scalar2=None, op0=ALU.mult),
                     reads=["xt%d" % s, "st%d" % s], writes=["xs%d" % s])
                for half in range(2):
                    pv = pt[half].rearrange("p (k m) -> p k m", m=128)

                    def fn(e, half=half, pv=pv):
                        ins = None
                        for k in range(8):
                            kc = half * 8 + k
                            ins = e.transpose(pv[:, k, :], xs[s][:, kc * 128:(kc + 1) * 128], idb[:])
                        return ins
                    S.op("pe", fn, reads=["xs%d" % s, "idb"], writes=["pt%d" % half])
                    evac(hT[:, half * 8:half * 8 + 8, t * 128:(t + 1) * 128], pv, ["pt%d" % half], ["hT"])

            nxt_tile = 0
            for cb in range(12):
                slot = cb % 2
                col = cb * 512
                pairs = []
                for kh in range(2):
                    sl = (2 * cb + kh) % 4
                    pairs += [(cT[:, kh * 8 + k, :], wm[sl][:, k, :]) for k in range(8)]
                pairs.append((ones32[0:1, 0:33], bmc[slot][0:1, :]))
                mm(S, pa[slot][0:33, :], pairs, reads=["cT0", "cT1", wmn[(2 * cb) % 4], wmn[(2 * cb + 1) % 4], "bmc%d" % slot, "ones32"],
                   writes=["pa%d" % slot])
                if cb + 2 < 12:
                    load_wm0(cb + 2)
                isscale = (D <= col < 2 * D)
                S.op("act", lambda e, slot=slot, isscale=isscale: e.activation(
                    out=mrow[slot][:], in_=pa[slot][0:33, :], func=AF.Identity, bias=(ones32[0:33, 0:1] if isscale else 0.0)),
                     reads=["pa%d" % slot, "ones32"], writes=["mrow%d" % slot])
                if col >= 2 * D:
                    S.dma("sp", "gs%d" % slot, lambda e, slot=slot, col=col: e.dma_start(out=gscr[0:1, col - 2 * D:col - 2 * D + 512], in_=mrow[slot][0:1, :]),
                          reads=["mrow%d" % slot], writes=["gscr"])
                else:
                    w_i = 0 if col < D else 1
                    kc0 = (col % D) // 128
                    pv = pb[slot][:, 0:132].rearrange("p (k m) -> p k m", m=33)

                    def fn(e, slot=slot, pv=pv):
                        ins = None
                        for k in range(4):
                            ins = e.transpose(pv[:, k, :], mrow[slot][0:33, k * 128:(k + 1) * 128], id32[0:33, 0:33])
                        return ins
                    S.op("pe", fn, reads=["mrow%d" % slot, "id32"], writes=["pb%d" % slot])
                    S.op("dve", lambda e, pv=pv, w_i=w_i, kc0=kc0: e.tensor_copy(out=modT[:, w_i, kc0:kc0 + 4, :], in_=pv),
                         reads=["pb%d" % slot], writes=["modT"])
                while nxt_tile < 18 and nxt_tile < (cb + 1) * 18 // 12:
                    x_tile(nxt_tile)
                    nxt_tile += 1
            while nxt_tile < 18:
                x_tile(nxt_tile)
                nxt_tile += 1
            load_w(WA, "WA", 0)
            load_w(WB, "WB", 512)
            for kc in range(16):
                for c0, c1, w_c in ((0, SEQ, 0), (SEQ, NT, 32)):
                    if (kc + (c0 > 0)) % 2 == 0:
                        S.op("dve", lambda e, kc=kc, c0=c0, c1=c1, w_c=w_c: e.tensor_scalar(
                            out=hT[:, kc, c0:c1], in0=hT[:, kc, c0:c1], scalar1=modT[:, 1, kc, w_c:w_c + 1], scalar2=modT[:, 0, kc, w_c:w_c + 1],
                            op0=ALU.mult, op1=ALU.add), reads=["hT", "modT"], writes=["hTm%d_%d" % (kc, c0)])
                    else:
                        S.op("act", lambda e, kc=kc, c0=c0, c1=c1, w_c=w_c: e.activation(
                            out=hT[:, kc, c0:c1], in_=hT[:, kc, c0:c1], func=AF.Identity, scale=modT[:, 1, kc, w_c:w_c + 1], bias=modT[:, 0, kc, w_c:w_c + 1]),
                            reads=["hT", "modT"], writes=["hTm%d_%d" % (kc, c0)])
            S.barrier()
        if debug:
            S.dma("sp", "dbg", lambda e: e.dma_start(out=dbg_ht, in_=hT[:].rearrange("p k n -> p (k n)")), reads=["hT"])
        if stop_after == "P1":
            S.finish("sp")
            mid.close()
            return nc

        with ExitStack() as ph:
            U = T(ph, "U", [128, 16, FW], BF16)
            NW = 257
            CS = [[T(ph, "cs%d_%d" % (i, j), [128, 16, NW], BF16) for j in range(2)] for i in range(2)]
            Y1 = T(ph, "Y1", [128, 8, 2, NW], BF16)
            ccs = T(ph, "ccs", [128, 2, 256], BF16)
            nscs = T(ph, "nscs", [128, 2, 256], BF16)
            Bsb = [T(ph, "Bsb%d" % i, [128, NW], F32) for i in range(2)]
            szf = [T(ph, "szf%d" % i, [128, 512], F32) for i in range(2)]
            tmpf = [T(ph, "tmpf%d" % i, [128, 2, 256], F32) for i in range(2)]
            fzs = [T(ph, "fzs%d" % i, [128, 2, 8, 256], BF16) for i in range(2)]
            ptZ = [p.bitcast(F32) for p in pt]
            S.dma("sp", "c0", lambda e: e.dma_start(out=ccs[:], in_=cc.rearrange("(k p) n -> p k n", p=128)), writes=["ccs"])
            S.dma("sp", "c1", lambda e: e.dma_start(out=nscs[:], in_=nsc.rearrange("(k p) n -> p k n", p=128)), writes=["nscs"])
            cn_v = cn.rearrange("(t p) n -> p t n", p=128)
            sn_v = sn.rearrange("(t p) n -> p t n", p=128)
            wbufs = [(WA, "WA"), (WB, "WB")]
            k = 0
            for cb in range(2):
                wb, wn = wbufs[cb]
                for t in range(16):
                    slot = k % 2
                    k += 1
                    mm(S, pa[slot][:, :], [(hT[:, kc, t * 128:(t + 1) * 128], wb[:, kc, :]) for kc in range(16)],
                       reads=["hT", wn], writes=["pa%d" % slot])
                    evac(U[:, t, cb * 512:(cb + 1) * 512], pa[slot][:, :], ["pa%d" % slot], ["U"])
            load_w(WA, "WA", OFF_ZF)
            load_w(WB, "WB", OFF_ZF + 512)
            fzt_v3 = fzt.rearrange("(c p) n -> p c n", p=128)
            for b in range(4):
                s = b % 2
                n0 = b * 256
                m0 = (7 - b) * 256
                S.dma("sp", "cs%d_0" % s, lambda e, s=s, n0=n0: e.dma_start(out=CS[s][0][:], in_=cn_v[:, :, n0:n0 + NW]),
                      writes=["cs%d_0" % s])
                S.dma("sp", "cs%d_1" % s, lambda e, s=s, n0=n0: e.dma_start(out=CS[s][1][:], in_=sn_v[:, :, n0:n0 + NW]),
                      writes=["cs%d_1" % s])
                for c in range(8):
                    for tr in range(2):
                        mm(S, pa[tr][:, 0:NW],
                           [(U[:, t, c * 128:(c + 1) * 128], CS[s][tr][:, t, :]) for t in range(16)],
                           reads=["U", "cs%d_%d" % (s, tr)], writes=["pa%d" % tr])
                    S.op("act", lambda e, c=c: e.activation(out=Y1[:, c, 0, :], in_=pa[0][:, 0:NW], func=AF.Copy),
                         reads=["pa0"], writes=["Y1_%d_0" % c])
                    S.op("dve", lambda e, c=c: e.tensor_copy(out=Y1[:, c, 1, :], in_=pa[1][:, 0:NW]),
                         reads=["pa1"], writes=["Y1_%d_1" % c])
                for g in range(4):
                    for cp in range(2):
                        c = 2 * g + cp
                        k = c % 2
                        pA, nA = (pb[0], "pb0A") if k == 0 else (pb[2], "pb2A")
                        pB, nB = (pb[1], "pb1B") if k == 0 else (pb[3], "pb3B")
                        pZ, nZ = ptZ[k], "pt%dZ" % k
                        mm(S, pA[:, 0:NW], [(ccs[:, ci, cp * 128:(cp + 1) * 128], Y1[:, 2 * g + ci, 0, :]) for ci in range(2)],
                           reads=["ccs", "Y1_%d_0" % (2 * g), "Y1_%d_0" % (2 * g + 1)], writes=[nA])
                        mm(S, pB[:, 0:NW], [(nscs[:, ci, cp * 128:(cp + 1) * 128], Y1[:, 2 * g + ci, 1, :]) for ci in range(2)],
                           reads=["nscs", "Y1_%d_1" % (2 * g), "Y1_%d_1" % (2 * g + 1)], writes=[nB])
                        wb, wn = wbufs[c // 4]
                        wsl = slice((c % 4) * 128, (c % 4 + 1) * 128)
                        mm(S, pZ[:, 0:256], [(wb[:, kc, wsl], hT[:, kc, n0:n0 + 256]) for kc in range(16)], reads=["hT", wn], writes=[nZ])
                        mm(S, pZ[:, 256:512], [(wb[:, kc, wsl], hT[:, kc, m0:m0 + 256]) for kc in range(16)], reads=["hT", wn], writes=[nZ])
                        S.op("act", lambda e, k=k, pB=pB: e.activation(out=Bsb[k][:], in_=pB[:, 0:NW], func=AF.Copy),
                             reads=[nB], writes=["Bsb%d" % k])
                        S.op("act", lambda e, k=k, pZ=pZ: e.activation(out=szf[k][:], in_=pZ[:, 0:512], func=AF.Silu),
                             reads=[nZ], writes=["szf%d" % k])
                        S.op("dve", lambda e, k=k, pA=pA: e.tensor_tensor(out=tmpf[k][:, 0, :], in0=pA[:, 0:256], in1=Bsb[k][:, 0:256], op=ALU.add),
                             reads=[nA, "Bsb%d" % k], writes=["tmpf%d_0" % k])
                        S.op("dve", lambda e, k=k, pA=pA: e.tensor_tensor(out=tmpf[k][:, 1, :], in0=pA[:, 256:0:-1], in1=Bsb[k][:, 256:0:-1], op=ALU.subtract),
                             reads=[nA, "Bsb%d" % k], writes=["tmpf%d_1" % k])
                        S.op("dve", lambda e, k=k, s=s, c=c: e.tensor_tensor(out=fzs[s][:, :, c, :], in0=tmpf[k][:], in1=szf[k][:].rearrange("p (a n) -> p a n", a=2), op=ALU.mult),
                             reads=["tmpf%d_0" % k, "tmpf%d_1" % k, "szf%d" % k], writes=["fzs%d" % s])
                S.dma("sp", "fzo%d" % s, lambda e, s=s, n0=n0: e.dma_start(out=fzt_v3[:, :, n0:n0 + 256], in_=fzs[s][:, 0, :, :]),
                      reads=["fzs%d" % s], writes=["fzt"])
                S.dma("sp", "fzm%d" % s, lambda e, s=s, m0=m0: e.dma_start(out=fzt_v3[:, :, m0:m0 + 256], in_=fzs[s][:, 1, :, :]),
                      reads=["fzs%d" % s], writes=["fzt"])
            S.barrier()
        if stop_after == "PF":
            S.finish("sp")
            mid.close()
            return nc

        with ExitStack() as ph:
            tabs = {n: T(ph, "tb_" + n, [128, 18, 64], BF16) for n in
                    ("Aq", "Bq", "Cq", "Dq", "Ak", "Bk", "Ck", "Dk")}
            load_w(WA, "WA", OFF_Q)
            load_w(WB, "WB", OFF_K)
            with ExitStack() as ph2:
                rc = T(ph2, "rc", [128, 18, 64], F32)
                rsn = T(ph2, "rsn", [128, 18, 64], F32)
                gbc = T(ph2, "gbc", [128, 256], F32)
                S.dma("sp", "c0", lambda e: e.dma_start(out=rc[:], in_=ropec.rearrange("(t p) f -> p t f", p=128)), writes=["rc"])
                S.dma("sp", "c1", lambda e: e.dma_start(out=rsn[:], in_=ropes.rearrange("(t p) f -> p t f", p=128)), writes=["rsn"])
                S.dma("sp", "c2", lambda e: e.dma_start(out=gbc[:], in_=qk_gain.partition_broadcast(128)[:, 0, :]), writes=["gbc"])
                scale_q = float(DH) ** -0.5
                S.op("dve", lambda e: e.tensor_scalar(out=gbc[:, 0:128], in0=gbc[:, 0:128], scalar1=scale_q, scalar2=None, op0=ALU.mult),
                     reads=["gbc"], writes=["gbc"])
                for pre, goff in (("q", 0), ("k", 128)):
                    gv = gbc[:, goff:goff + 128].rearrange("p (a h f) -> p a h f", a=2, h=2)
                    g0 = gv[:, :, 0, :]
                    g1 = gv[:, :, 1, :]
                    for name, g, tab, tabn in (("A", g0, rc, "rc"), ("B", g1, rsn, "rsn"), ("C", g1, rc, "rc"), ("D", g0, rsn, "rsn")):
                        dst = tabs[name + pre]

                        def fn(e, dst=dst, g=g, tab=tab):
                            return e.tensor_tensor(
                                out=dst[:].rearrange("p t (a f) -> p t a f", a=2),
                                in0=tab[:].rearrange("p t (a f) -> p t a f", a=2),
                                in1=g.unsqueeze(1).broadcast_to([128, 18, 2, 32]), op=ALU.mult)
                        S.op("dve", fn, reads=[tabn, "gbc"], writes=["tb_" + name + pre])
                S.barrier()
            QT = T(ph, "QT", [128, 4, SEQ], BF16)
            KT = T(ph, "KT", [128, 4, NT], BF16)
            V = T(ph, "V", [128, 18, 512], BF16)
            scr = [T(ph, "scr0", [128, 512], BF16)]
            qn = [T(ph, "qn%d" % i, [128, 512], F32) for i in range(2)]
            qr = [T(ph, "qr%d" % i, [128, 512], BF16) for i in range(2)]
            rt = [T(ph, "rt%d" % i, [128, 4, 256], F32) for i in range(2)]
            ss = [T(ph, "ss%d" % i, [128, 12], F32) for i in range(2)]
            TB = [T(ph, "TB%d" % i, [128, 2 * NI * 64], BF16) for i in range(2)]
            exb = [T(ph, "exb%d" % i, [128, 512], BF16) for i in range(5)]
            ptS = [p.bitcast(F32) for p in pt]
            sza = [T(ph, "sza0", [128, 256], F32)]
            rden = [T(ph, "rden0", [128, 256], F32)]
            o1 = [T(ph, "o1_0", [128, 256], F32)]
            ozs = [T(ph, "ozs%d" % i, [128, 256], BF16) for i in range(2)]

            tile_i = [0]

            def qk_tile(ps, psn, pre, t, dstT):
                s = tile_i[0] % 2
                tile_i[0] += 1
                S.op("act", lambda e: e.activation(out=scr[0][:], in_=ps, func=AF.Square), reads=[psn], writes=["scr0"])
                S.op("dve", lambda e: e.tensor_reduce(out=ss[s][:, 0:4], in_=scr[0][:].rearrange("p (h d) -> p h d", h=4), axis=AX.X, op=ALU.add),
                     reads=["scr0"], writes=["ss%d" % s])
                S.op("act", lambda e: e.activation(out=ss[s][:, 4:8], in_=ss[s][:, 0:4], func=AF.Ln, scale=1.0 / DH, bias=epst[:]),
                     reads=["ss%d" % s, "epst"], writes=["ss%d" % s])
                S.op("act", lambda e: e.activation(out=ss[s][:, 8:12], in_=ss[s][:, 4:8], func=AF.Exp, scale=-0.5),
                     reads=["ss%d" % s], writes=["ss%d" % s])
                S.op("dve", lambda e: e.tensor_tensor(out=qn[s][:].rearrange("p (h d) -> p h d", h=4), in0=ps.rearrange("p (h d) -> p h d", h=4),
                                                      in1=ss[s][:, 8:12].unsqueeze(2).broadcast_to([128, 4, 128]), op=ALU.mult),
                     reads=[psn, "ss%d" % s], writes=["qn%d" % s])
                qv = qn[s][:].rearrange("p (h a b f) -> p h a b f", h=4, a=2, b=2)
                ov = qr[s][:].rearrange("p (h a b f) -> p h a b f", h=4, a=2, b=2)
                x0 = qv[:, :, :, 0, :]
                x1 = qv[:, :, :, 1, :]
                tb = lambda n: tabs[n + pre][:, t, :].rearrange("p (a f) -> p a f", a=2).unsqueeze(1).broadcast_to([128, 4, 2, 32])
                rv = [rt[s][:, i, :].rearrange("p (h a f) -> p h a f", h=4, a=2) for i in range(4)]
                tn = ["tb_" + n + pre for n in "ABCD"]
                qnn = "qn%d" % s
                rn = ["rt%d_%d" % (s, i) for i in range(4)]
                S.op("dve", lambda e: e.tensor_tensor(out=rv[0], in0=x0, in1=tb("A"), op=ALU.mult), reads=[qnn, tn[0]], writes=[rn[0]])
                S.op("dve", lambda e: e.tensor_tensor(out=rv[1], in0=x1, in1=tb("B"), op=ALU.mult), reads=[qnn, tn[1]], writes=[rn[1]])
                S.op("dve", lambda e: e.tensor_tensor(out=ov[:, :, :, 0, :], in0=rv[0], in1=rv[1], op=ALU.subtract),
                     reads=[rn[0], rn[1]], writes=["qr%d_a" % s])
                S.op("pool", lambda e: e.tensor_tensor(out=rv[2], in0=x1, in1=tb("C"), op=ALU.mult), reads=[qnn, tn[2]], writes=[rn[2]])
                S.op("pool", lambda e: e.tensor_tensor(out=rv[3], in0=x0, in1=tb("D"), op=ALU.mult), reads=[qnn, tn[3]], writes=[rn[3]])
                S.op("pool", lambda e: e.tensor_tensor(out=ov[:, :, :, 1, :], in0=rv[2], in1=rv[3], op=ALU.add),
                     reads=[rn[2], rn[3]], writes=["qr%d_b" % s])
                pv = pt[s][:, 0:512].rearrange("p (k m) -> p k m", m=128)

                def fin():
                    def fn(e):
                        ins = None
                        for hh in range(4):
                            ins = e.transpose(pv[:, hh, :], qr[s][:, hh * 128:(hh + 1) * 128], idb[:])
                        return ins
                    S.op("pe", fn, reads=["qr%d_a" % s, "qr%d_b" % s, "idb"], writes=["pt%d" % s])
                    evac(dstT[:, :, t * 128:(t + 1) * 128], pv, ["pt%d" % s], [pre + "T"])
                return fin

            def load_tb(h, hs):
                hw = NI * 64
                for tabi in range(2):
                    stg = rt[tabi][:].rearrange("p a b -> p (a b)")[:, 0:hw]
                    rn = ["rt%d_%d" % (tabi, i) for i in range(4)]
                    S.dma("sp", "tbs%d" % tabi, lambda e, stg=stg, tabi=tabi: e.dma_start(out=stg, in_=btab[h][:, tabi * hw:(tabi + 1) * hw]),
                          writes=rn)
                    S.op("act", lambda e, stg=stg, tabi=tabi: e.activation(out=TB[hs][:, tabi * hw:(tabi + 1) * hw], in_=stg, func=AF.Exp),
                         reads=rn, writes=["TB%d" % hs])

            def attention_head(hl, h, hs):
                def prologue():
                    if h + 1 < H:
                        load_tb(h + 1, (hs + 1) % 2)
                tbv = TB[hs][:].rearrange("p (t i c) -> p t i c", t=2, i=NI)
                units = []
                for a in range(8):
                    chunks = [("loc", j) for j in ATT_BLOCKS[a]] + [("ctx", 16), ("ctx", 17)]
                    for ci, (kind, j) in enumerate(chunks):
                        units.append((a, ci, len(chunks), kind, j))
                npair = len(units) // 2
                banks = (pa[0][:], pa[1][:], pb[0][:], ptS[0], ptS[1])
                bnames = ("pa0S", "pa1S", "pb0S", "pt0S", "pt1S")

                def front(pi):
                    sb = pi % 5
                    bank, bname, en = banks[sb], bnames[sb], "exb%d" % sb

                    def fn(e):
                        ins = None
                        for k in range(2):
                            a, ci, nch, kind, j = units[2 * pi + k]
                            ins = e.matmul(bank[:, k * 256:(k + 1) * 256], KT[:, hl, j * 128:(j + 1) * 128],
                                           QT[:, hl, a * 256:(a + 1) * 256], start=True, stop=True)
                        return ins
                    S.op("pe", fn, reads=["kT", "qT"], writes=[bname])
                    S.op("act", lambda e: e.activation(out=exb[sb][:], in_=bank[:, 0:512], func=AF.Exp), reads=[bname], writes=[en])
                    for k in range(2):
                        a, ci, nch, kind, j = units[2 * pi + k]
                        if kind == "loc":
                            tab = 1 if a in (0, 7) else 0
                            i0 = 4 * a - 2 * j + IOFF
                            ev = exb[sb][:, k * 256:(k + 1) * 256].rearrange("p (i c) -> p i c", i=4)
                            S.op("dve", lambda e, ev=ev, tab=tab, i0=i0: e.tensor_tensor(out=ev, in0=ev, in1=tbv[:, tab, i0:i0 + 4, :], op=ALU.mult),
                                 reads=[en, "TB%d" % hs], writes=[en])

                def back(pi):
                    sb = pi % 5
                    en = "exb%d" % sb
                    poO = pb[1][:, 0:256]
                    poD = pb[2][:, 0:256]

                    def fn(e):
                        ins = None
                        for k in range(2):
                            a, ci, nch, kind, j = units[2 * pi + k]
                            pm = exb[sb][:, k * 256:(k + 1) * 256]
                            e.matmul(poO, V[:, j, hl * 128:(hl + 1) * 128], pm, start=(ci == 0), stop=(ci == nch - 1))
                            ins = e.matmul(poD, onesb[:], pm, start=(ci == 0), stop=(ci == nch - 1))
                        return ins
                    S.op("pe", fn, reads=["V", en, "onesb"], writes=["pb1O", "pb2D"])
                    a, ci, nch, kind, j = units[2 * pi + 1]
                    if ci == nch - 1:
                        bs = a % 2
                        mm(S, pb[3][:, 0:256], [(WB[:, kc, hl * 128:(hl + 1) * 128], hT[:, kc, a * 256:(a + 1) * 256]) for kc in range(16)],
                           reads=["hT", "WB"], writes=["pb3Z"])
                        S.op("act", lambda e: e.activation(out=rden[0][:], in_=poD, func=AF.Copy),
                             reads=["pb2D"], writes=["rden0"])
                        S.op("act", lambda e: e.activation(out=o1[0][:], in_=poO, func=AF.Copy, scale=0.5),
                             reads=["pb1O"], writes=["o1_0"])
                        S.op("act", lambda e: e.activation(out=sza[0][:], in_=pb[3][:, 0:256], func=AF.Tanh, scale=0.5),
                             reads=["pb3Z"], writes=["sza0"])
                        S.op("dve", lambda e: e.reciprocal(out=rden[0][:], in_=rden[0][:]),
                             reads=["rden0"], writes=["rden0"])
                        S.op("dve", lambda e: e.scalar_tensor_tensor(out=sza[0][:], in0=sza[0][:], scalar=1.0, in1=pb[3][:, 0:256], op0=ALU.add, op1=ALU.mult),
                             reads=["sza0", "pb3Z"], writes=["sza0"])
                        S.op("pool", lambda e: e.tensor_tensor(out=o1[0][:], in0=o1[0][:], in1=rden[0][:], op=ALU.mult),
                             reads=["o1_0", "rden0"], writes=["o1_0"])
                        S.op("pool", lambda e: e.tensor_tensor(out=ozs[bs][:], in0=o1[0][:], in1=sza[0][:], op=ALU.mult),
                             reads=["o1_0", "sza0"], writes=["ozs%d" % bs])
                        S.dma("sp", "ozo%d" % bs, lambda e: e.dma_start(out=ozt[h * 128:(h + 1) * 128, a * 256:(a + 1) * 256], in_=ozs[bs][:]),
                              reads=["ozs%d" % bs], writes=["ozt"])
                fr = [(lambda pi=pi: front(pi)) for pi in range(npair)]
                fr[0] = (lambda: (prologue(), front(0)))
                bk = [(lambda pi=pi: back(pi)) for pi in range(npair)]
                return fr, bk

            hcount = 0
            load_tb(0, 0)
            for g in range(4):
                if g > 0:
                    load_w(WB, "WB", OFF_K + g * 512)
                k = 0
                prev = None
                for pre, wbuf, wn, dst, ntile in (("q", WA, "WA", QT, 16), ("k", WB, "WB", KT, 18)):
                    for t in range(ntile):
                        slot = k % 2
                        k += 1
                        mm(S, pa[slot][:, :], [(hT[:, kc, t * 128:(t + 1) * 128], wbuf[:, kc, :]) for kc in range(16)],
                           reads=["hT", wn], writes=["pa%d" % slot])
                        fin = qk_tile(pa[slot][:, :], "pa%d" % slot, pre, t, dst)
                        if not _DEFER:
                            fin()
                            continue
                        if prev is not None:
                            prev()
                        prev = fin
                    if pre == "q":
                        load_w(WA, "WA", OFF_V + g * 512)
                if prev is not None:
                    prev()
                load_w(WB, "WB", OFF_ZA + g * 512)
                for t in range(18):
                    slot = k % 2
                    k += 1
                    mm(S, pa[slot][:, :], [(hT[:, kc, t * 128:(t + 1) * 128], WA[:, kc, :]) for kc in range(16)],
                       reads=["hT", "WA"], writes=["pa%d" % slot])
                    evac(V[:, t, :], pa[slot][:, :], ["pa%d" % slot], ["V"])
                if g + 1 < 4:
                    load_w(WA, "WA", OFF_Q + (g + 1) * 512)
                fronts, backs = [], []
                for hl in range(4):
                    fr, bk = attention_head(hl, g * 4 + hl, hcount % 2)
                    fronts += fr
                    backs += bk
                    hcount += 1
                for n in range(len(fronts) + _LA):
                    if n < len(fronts):
                        fronts[n]()
                    if n >= _LA:
                        backs[n - _LA]()
            if debug:
                dq = nc.dram_tensor("dbg_qt", [128, 4 * SEQ], BF16, kind="ExternalOutput").ap()
                dk = nc.dram_tensor("dbg_kt", [128, 4 * NT], BF16, kind="ExternalOutput").ap()
                dv = nc.dram_tensor("dbg_v", [128, 18 * 512], BF16, kind="ExternalOutput").ap()
                S.dma("sp", "dbg", lambda e: e.dma_start(out=dq, in_=QT[:].rearrange("p k n -> p (k n)")), reads=["qT"])
                S.dma("sp", "dbg", lambda e: e.dma_start(out=dk, in_=KT[:].rearrange("p k n -> p (k n)")), reads=["kT"])
                S.dma("sp", "dbg", lambda e: e.dma_start(out=dv, in_=V[:].rearrange("p k n -> p (k n)")), reads=["V"])
            S.barrier()
        if stop_after == "PA":
            S.finish("sp")
            mid.close()
            return nc

        with ExitStack() as ph:
            OZ = T(ph, "OZ", [128, 16, SEQ], BF16)
            FZb = [T(ph, "FZb%d" % i, [128, 8, 512], BF16) for i in range(2)]
            WM = [w[:].rearrange("p k n -> p (k n)")[:, 0:7168].rearrange("p (k n) -> p k n", n=128) for w in (WA, WB)]
            sg = [T(ph, "sg%d" % i, [128, 2, 512], F32) for i in range(2)]
            ty = [T(ph, "ty%d" % i, [128, 2, 512], F32) for i in range(2)]
            ys = [T(ph, "ys%d" % i, [128, 512], BF16) for i in range(2)]
            wf_v = w_f_out.rearrange("(kc p) n -> p kc n", p=128)
            wa_v = w_a_out.rearrange("(kc p) n -> p kc n", p=128)
            ozt_v = ozt.rearrange("(c p) n -> p c n", p=128)
            fzt_v = fzt.rearrange("(c p) n -> p c n", p=128)

            def load_wm(slot, i):
                c0 = i * 128
                S.dma("pool", "wm%da" % slot, lambda e: e.dma_start(out=WM[slot][:, 0:8, :], in_=wf_v[:, :, c0:c0 + 128]), writes=["WM%d" % slot])
                S.dma("pool", "wm%db" % slot, lambda e: e.dma_start(out=WM[slot][:, 8:24, :], in_=wa_v[:, :, c0:c0 + 128]), writes=["WM%d" % slot])
                S.dma("pool", "wm%dc" % slot, lambda e: e.dma_start(out=WM[slot][:, 24:40, :], in_=w_in_v[:, :, OFF_GF + c0:OFF_GF + c0 + 128]), writes=["WM%d" % slot])
                S.dma("pool", "wm%dd" % slot, lambda e: e.dma_start(out=WM[slot][:, 40:56, :], in_=w_in_v[:, :, OFF_GA + c0:OFF_GA + c0 + 128]), writes=["WM%d" % slot])

            cnt = 0
            pairs3 = [(pb[0], pb[1], "pb0", "pb1"), (pb[2], pb[3], "pb2", "pb3"), (pa[0], pa[1], "pa0", "pa1")]
            item = [0]
            for tb in range(4):
                S.dma("sp", "ozl%d" % tb, lambda e, tb=tb: e.dma_start(out=OZ[:, :, tb * 512:(tb + 1) * 512], in_=ozt_v[:, :, tb * 512:(tb + 1) * 512]),
                      reads=["ozt"], writes=["OZ%d" % tb])
            load_wm(0, 0)
            if True:
                for i in range(16):
                    ws = cnt % 2
                    cnt += 1
                    if i + 1 < 16:
                        load_wm(cnt % 2, i + 1)
                    W = WM[ws]
                    wn = "WM%d" % ws
                    for tb in range(4):
                        s = (i * 4 + tb) % 2
                        n0 = tb * 512
                        t0 = 0
                        tok = slice(n0, n0 + 512)
                        if i == 0 and tb == 0:
                            S.dma("sp", "fzl0", lambda e: e.dma_start(out=FZb[0][:], in_=fzt_v[:, :, 0:512]), reads=["fzt"], writes=["FZb0"])
                        nxt = i * 4 + tb + 1
                        if nxt < 64:
                            S.dma("sp", "fzl%d" % (nxt % 2), lambda e, nxt=nxt: e.dma_start(out=FZb[nxt % 2][:], in_=fzt_v[:, :, (nxt % 4) * 512:(nxt % 4 + 1) * 512]),
                                  reads=["fzt"], writes=["FZb%d" % (nxt % 2)])
                        for br in range(2):
                            pY, pG, nY, nG = pairs3[item[0] % 3]
                            item[0] += 1
                            if br == 0:
                                mm(S, pG[:, :], [(W[:, 24 + kc, :], hT[:, kc, tok]) for kc in range(16)], reads=[wn, "hT"], writes=[nG])
                                mm(S, pY[:, :], [(W[:, kc, :], FZb[s][:, kc, :]) for kc in range(8)], reads=[wn, "FZb%d" % s], writes=[nY])
                            else:
                                mm(S, pG[:, :], [(W[:, 40 + kc, :], hT[:, kc, tok]) for kc in range(16)], reads=[wn, "hT"], writes=[nG])
                                mm(S, pY[:, :], [(W[:, 8 + kc, :], OZ[:, kc, n0:n0 + 512]) for kc in range(16)], reads=[wn, "OZ%d" % tb], writes=[nY])
                            S.op("act", lambda e, s=s, pG=pG, br=br: e.activation(out=sg[s][:, br, :], in_=pG[:, :], func=AF.Sigmoid),
                                 reads=[nG], writes=["sg%d_%d" % (s, br)])
                            S.op("dve", lambda e, s=s, pY=pY, br=br: e.tensor_tensor(out=ty[s][:, br, :], in0=pY[:, :], in1=sg[s][:, br, :], op=ALU.mult),
                                 reads=[nY, "sg%d_%d" % (s, br)], writes=["ty%d_%d" % (s, br)])
                        S.op("dve", lambda e, s=s: e.tensor_tensor(out=ys[s][:], in0=ty[s][:, 0, :], in1=ty[s][:, 1, :], op=ALU.add),
                             reads=["ty%d_0" % s, "ty%d_1" % s], writes=["ys%d" % s])
                        S.dma("sp", "yo%d" % s, lambda e, s=s, i=i, tok=tok: e.dma_start(out=yt[i * 128:(i + 1) * 128, tok], in_=ys[s][:]),
                              reads=["ys%d" % s], writes=["yt"])
            S.barrier()
        if stop_after == "M1":
            S.finish("sp")
            mid.close()
            return nc
        mid.close()

        with ExitStack() as ph:
            WOb = [T(ph, "WOb%d" % i, [128, 16, 512], BF16) for i in range(2)]
            gate = T(ph, "gate", [128, D], F32)
            yb = [T(ph, "yb%d" % i, [128, 16, 512], BF16) for i in range(2)]
            xo = [T(ph, "xo%d" % i, [128, 512], F32) for i in range(3)]
            ot = [T(ph, "ot%d" % i, [128, 512], F32) for i in range(3)]
            tg = [T(ph, "tg%d" % i, [128, 512], F32) for i in range(2)]
            wo_v = w_out.rearrange("(kc p) n -> p kc n", p=128)
            yt_v = yt.rearrange("(c p) n -> p c n", p=128)
            banks = [(pa[0], "pa0"), (pa[1], "pa1"), (pb[0], "pb0"), (pb[1], "pb1")]

            def load_wo(cb):
                S.dma("pool", "wo%d" % (cb % 2), lambda e: e.dma_start(out=WOb[cb % 2][:], in_=wo_v[:, :, cb * 512:(cb + 1) * 512]),
                      writes=["WOb%d" % (cb % 2)])

            def load_yb(j):
                tb = j % 4
                S.dma("sp", "ybl%d" % (j % 2), lambda e: e.dma_start(out=yb[j % 2][:], in_=yt_v[:, :, tb * 512:(tb + 1) * 512]),
                      reads=["yt"], writes=["yb%d" % (j % 2)])

            def load_x(k):
                cb, t = k // 16, k % 16
                S.dma("sp", "xl%d" % (k % 3), lambda e: e.dma_start(out=xo[k % 3][:], in_=x[t * 128:(t + 1) * 128, cb * 512:(cb + 1) * 512]),
                      writes=["xo%d" % (k % 3)])
            load_wo(0)
            load_wo(1)
            S.dma("sp", "gl", lambda e: e.dma_start(out=gate[:], in_=gscr.partition_broadcast(128)[:, 0, :]), reads=["gscr"], writes=["gate"])
            load_yb(0)
            load_x(0)
            for k in range(64):
                cb, t = k // 16, k % 16
                tb, tt = t // 4, t % 4
                j = cb * 4 + tb
                if tt == 0 and j + 1 < 16:
                    load_yb(j + 1)
                if k + 1 < 64:
                    load_x(k + 1)
                bank, bn = banks[k % 4]
                mm(S, bank[:, :], [(yb[j % 2][:, kc, tt * 128:(tt + 1) * 128], WOb[cb % 2][:, kc, :]) for kc in range(16)],
                   reads=["yb%d" % (j % 2), "WOb%d" % (cb % 2)], writes=[bn])
                if t == 15 and cb + 2 < 4:
                    load_wo(cb + 2)
                S.op("dve", lambda e, bank=bank, k=k, cb=cb: e.tensor_tensor(out=tg[k % 2][:], in0=bank[:, :], in1=gate[:, cb * 512:(cb + 1) * 512], op=ALU.mult),
                     reads=[bn, "gate"], writes=["tg%d" % (k % 2)])
                S.op("dve", lambda e, k=k: e.tensor_tensor(out=ot[k % 3][:], in0=tg[k % 2][:], in1=xo[k % 3][:], op=ALU.add),
                     reads=["tg%d" % (k % 2), "xo%d" % (k % 3)], writes=["ot%d" % (k % 3)])
                S.dma("act", "oo%d" % (k % 3), lambda e, k=k, cb=cb, t=t: e.dma_start(out=out[t * 128:(t + 1) * 128, cb * 512:(cb + 1) * 512], in_=ot[k % 3][:]),
                      reads=["ot%d" % (k % 3)], writes=["out"])
            S.finish("sp")
    return nc


_CONSTS = None


def make_in_maps(x, c, ctx, c_ctx, w_mod, b_mod, w_in, q_gain, k_gain, rpb, w_f_out, w_a_out, w_out, cores=range(8)):
    global _CONSTS
    if _CONSTS is None:
        _CONSTS = host_consts()
    f = lambda a: np.ascontiguousarray(np.asarray(a, dtype=np.float32))
    x, c, ctx, c_ctx = f(x), f(c), f(ctx), f(c_ctx)
    shared = dict(
        w_mod=f(w_mod)[0], b_mod=f(b_mod)[0][None, :], w_in=f(w_in)[0],
        qk_gain=np.concatenate([f(q_gain)[0], f(k_gain)[0]])[None, :],
        btab=host_bias_tables(f(rpb)[0]).reshape(H, 128, 2 * NI * 64),
        w_f_out=f(w_f_out)[0], w_a_out=f(w_a_out)[0], w_out=f(w_out)[0], **_CONSTS)
    maps = []
    for b in cores:
        m = dict(shared)
        m["x"] = x[b]
        m["ctx"] = ctx[b]
        m["cvec"] = np.stack([c[b], c_ctx], 0)
        maps.append(m)
    return maps


def kernel(x, c, ctx, c_ctx, w_mod, b_mod, w_in, q_gain, k_gain, rpb, w_f_out, w_a_out, w_out):
    maps = make_in_maps(x, c, ctx, c_ctx, w_mod, b_mod, w_in, q_gain, k_gain, rpb, w_f_out, w_a_out, w_out)
    nc = build_nc()
    res = run_bass_kernel_spmd(nc, maps, core_ids=list(range(8)))
    return np.stack([np.asarray(r["out"], dtype=np.float32) for r in res.results], 0)
```

```python
import numpy as np
import ml_dtypes
from contextlib import ExitStack
import concourse.bass as bass
import concourse.mybir as mybir
from concourse.bass_utils import run_bass_kernel_spmd

F32 = mybir.dt.float32
BF16 = mybir.dt.bfloat16
AF = mybir.ActivationFunctionType
ALU = mybir.AluOpType
AX = mybir.AxisListType

D = 2048
SEQ = 2048
CTX = 256
NT = SEQ + CTX
H = 16
DH = 128
FW = 1024
OFF_ZF, OFF_Q, OFF_K, OFF_V, OFF_ZA, OFF_GF, OFF_GA = 1024, 2048, 4096, 6144, 8192, 10240, 12288
INW = 14336
EPS = 1e-6
NEG = -30000.0
_LA = 4
_DEFER = True


class Sched:
    def __init__(self, nc, stack):
        self.nc = nc
        self.stack = stack
        self.eng = {"pe": nc.tensor, "act": nc.scalar, "dve": nc.vector,
                    "pool": nc.gpsimd, "sp": nc.sync}
        self.esem = {}
        self.ecnt = {}
        for e in ("pe", "act", "dve", "pool"):
            self.esem[e] = stack.enter_context(nc.semaphore("es_" + e))
            self.ecnt[e] = 0
        self.waited = {e: {} for e in self.eng}
        self.lastw = {}
        self.readers = {}
        self.dsem = {}
        self.bank_last = {}

    @staticmethod
    def _bank(name):
        if len(name) >= 3 and name[:2] in ("pa", "pb", "pt") and name[2].isdigit():
            return name[:3]
        return None

    def _collect(self, reads, writes, eng=None):
        deps = []
        for n in tuple(reads) + tuple(writes):
            b = self._bank(n)
            if b is not None:
                for e2, ev in self.bank_last.get(b, {}).items():
                    if e2 != eng:
                        deps.append(ev)
        for r in reads:
            ev = self.lastw.get(r)
            if ev is not None:
                deps.append(ev)
        for w in writes:
            ev = self.lastw.get(w)
            if ev is not None:
                deps.append(ev)
            deps.extend(self.readers.get(w, ()))
        return deps

    def _emit_waits(self, eng, deps):
        e = self.eng[eng]
        best = {}
        for (key, sem, val, src) in deps:
            if eng == "pe" and src == "pe":
                continue
            if self.waited[eng].get(key, 0) >= val:
                continue
            if best.get(key, (None, 0))[1] < val:
                best[key] = (sem, val)
        for key, (sem, val) in best.items():
            e.wait_ge(sem, val)
            self.waited[eng][key] = val

    def _record(self, ev, reads, writes, eng=None):
        for n in tuple(reads) + tuple(writes):
            b = self._bank(n)
            if b is not None:
                self.bank_last.setdefault(b, {})[eng] = ev
        for w in writes:
            self.lastw[w] = ev
            self.readers[w] = []
        for r in reads:
            self.readers.setdefault(r, []).append(ev)

    def op(self, eng, fn, reads=(), writes=()):
        deps = self._collect(reads, writes, eng)
        self._emit_waits(eng, deps)
        ins = fn(self.eng[eng])
        self.ecnt[eng] += 1
        ins.then_inc(self.esem[eng], 1)
        ev = ("e_" + eng, self.esem[eng], self.ecnt[eng], eng)
        self._record(ev, reads, writes, eng)
        return ev

    def dma(self, queue, slot, fn, reads=(), writes=()):
        if slot not in self.dsem:
            self.dsem[slot] = [self.stack.enter_context(self.nc.semaphore("ds_" + slot)), 0]
        deps = self._collect(reads, writes)
        st = self.dsem[slot]
        if st[1] > 0:
            deps.append(("d_" + slot, st[0], st[1], "dma"))
        self._emit_waits(queue, deps)
        ins = fn(self.eng[queue])
        st[1] += 16
        ins.then_inc(st[0], 16)
        ev = ("d_" + slot, st[0], st[1], "dma")
        self._record(ev, reads, writes)
        return ev

    def _all_events(self):
        evs = []
        for e in self.esem:
            if self.ecnt[e] > 0:
                evs.append(("e_" + e, self.esem[e], self.ecnt[e], "x"))
        for slot, st in self.dsem.items():
            if st[1] > 0:
                evs.append(("d_" + slot, st[0], st[1], "dma"))
        return evs

    def barrier(self):
        evs = self._all_events()
        for eng in self.eng:
            self._emit_waits(eng, evs)
        self.lastw = {}
        self.readers = {}
        self.bank_last = {}

    def finish(self, eng="sp"):
        self._emit_waits(eng, self._all_events())


def mm(S, out, pairs, reads, writes):
    def fn(e):
        n = len(pairs)
        ins = None
        for i, (l, r) in enumerate(pairs):
            ins = e.matmul(out, l, r, start=(i == 0), stop=(i == n - 1))
        return ins
    return S.op("pe", fn, reads, writes)


def _rs(r):
    return min(max(r - 4, 0), 24)


def attn_geometry():
    blocks = []
    lo, hi = 10 ** 9, -10 ** 9
    for a in range(8):
        r0, r1 = 4 * a, 4 * a + 3
        j0 = _rs(r0) // 2
        j1 = (_rs(r1) + 7) // 2
        js = list(range(j0, j1 + 1))
        blocks.append(js)
        for j in js:
            lo = min(lo, r0 - 2 * j)
            hi = max(hi, r1 - 2 * j)
    ioff = -lo
    ni = hi - lo + 1
    return blocks, ioff, ni


ATT_BLOCKS, IOFF, NI = attn_geometry()


def host_bias_tables(rpb):
    rpb = np.asarray(rpb, np.float32)
    out = np.full((H, 128, 2, NI, 64), NEG, np.float32)
    c = np.arange(64)
    cs = np.clip(c - 8, 0, 48)
    kc = np.arange(64)
    colvalid = (kc[:, None] >= cs[None, :]) & (kc[:, None] <= cs[None, :] + 15)
    dc = kc[:, None] - c[None, :] + 15
    dcc = np.clip(dc, 0, 30)
    for hf in range(2):
        for i in range(NI):
            delta = hf - (i - IOFF)
            if abs(delta) > 7:
                continue
            vals = rpb[:, delta + 7, :][:, dcc]
            vals = np.where(colvalid[None], vals, NEG)
            out[:, hf * 64:(hf + 1) * 64, 1, i, :] = vals
            if -4 <= delta <= 3:
                out[:, hf * 64:(hf + 1) * 64, 0, i, :] = vals
    return out


def host_consts():
    n = np.arange(2048, dtype=np.float64)
    ang = 2 * np.pi * ((n[:, None] * n[None, :]) % 2048) / 2048.0
    cn = (np.cos(ang) / np.sqrt(2048.0)).astype(ml_dtypes.bfloat16)
    sn = (np.sin(ang) / np.sqrt(2048.0)).astype(ml_dtypes.bfloat16)
    m = np.arange(256, dtype=np.float64)
    angc = 2 * np.pi * ((m[:, None] * m[None, :]) % 256) / 256.0
    cc = (np.cos(angc) / 16.0).astype(ml_dtypes.bfloat16)
    nsc = (-np.sin(angc) / 16.0).astype(ml_dtypes.bfloat16)
    t = np.arange(2048)
    pos = np.stack([t // 64, t % 64], -1).astype(np.float32)
    inv = (10000.0 ** (-np.arange(32, dtype=np.float32) / 32)).astype(np.float32)
    a = pos[..., None] * inv
    rc = np.ones((NT, 64), np.float32)
    rs_ = np.zeros((NT, 64), np.float32)
    rc[:2048] = np.cos(a).reshape(2048, 64)
    rs_[:2048] = np.sin(a).reshape(2048, 64)
    return dict(cn=cn, sn=sn, cc=cc, nsc=nsc, ropec=rc, ropes=rs_,
                ident=np.eye(128).astype(ml_dtypes.bfloat16),
                ident32=np.eye(128).astype(np.float32))


def build_nc(debug=False, stop_after=None):
    nc = bass.Bass("TRN2", target_bir_lowering=False)
    dt_in = lambda n, s, d=F32: nc.dram_tensor(n, s, d, kind="ExternalInput").ap()
    x = dt_in("x", [SEQ, D])
    ctx = dt_in("ctx", [CTX, D])
    cvec = dt_in("cvec", [2, D])
    w_mod = dt_in("w_mod", [D, 3 * D])
    b_mod = dt_in("b_mod", [1, 3 * D])
    w_in = dt_in("w_in", [D, INW])
    qk_gain = dt_in("qk_gain", [1, 256])
    btab = dt_in("btab", [H, 128, 2 * NI * 64])
    w_f_out = dt_in("w_f_out", [FW, D])
    w_a_out = dt_in("w_a_out", [D, D])
    w_out = dt_in("w_out", [D, D])
    cn = dt_in("cn", [2048, 2048], BF16)
    sn = dt_in("sn", [2048, 2048], BF16)
    cc = dt_in("cc", [256, 256], BF16)
    nsc = dt_in("nsc", [256, 256], BF16)
    ropec = dt_in("ropec", [NT, 64])
    ropes = dt_in("ropes", [NT, 64])
    ident = dt_in("ident", [128, 128], BF16)
    ident32 = dt_in("ident32", [128, 128])
    out = nc.dram_tensor("out", [SEQ, D], F32, kind="ExternalOutput").ap()
    skind = dict(kind="ExternalOutput") if debug else {}
    fzt = nc.dram_tensor("fzt", [FW, SEQ], BF16, **skind).ap()
    ozt = nc.dram_tensor("ozt", [D, SEQ], BF16, **skind).ap()
    yt = nc.dram_tensor("yt", [D, SEQ], BF16, **skind).ap()
    gscr = nc.dram_tensor("gscr", [1, D], F32, **skind).ap()
    if debug:
        dbg_ht = nc.dram_tensor("dbg_ht", [128, 16 * NT], BF16, kind="ExternalOutput").ap()

    w_in_v = w_in.rearrange("(kc p) n -> p kc n", p=128)

    with ExitStack() as top:
        S = Sched(nc, top)
        T = lambda st, n, s, d: st.enter_context(nc.sbuf_tensor(n, s, d))
        pa = [top.enter_context(nc.psum_tensor("pa%d" % i, [128, 512], F32)) for i in range(2)]
        ptt = [top.enter_context(nc.psum_tensor("pt%d" % i, [128, 1024], BF16)) for i in range(2)]
        pt = [p[:] for p in ptt]
        pb = [top.enter_context(nc.psum_tensor("pb%d" % i, [128, 512], F32)) for i in range(4)]
        idb = T(top, "idb", [128, 128], BF16)
        id32 = T(top, "id32", [128, 128], F32)
        ones32 = T(top, "ones32", [128, 128], F32)
        onesb = T(top, "onesb", [128, 128], BF16)
        epst = T(top, "epst", [128, 1], F32)
        S.dma("sp", "c0", lambda e: e.dma_start(out=idb[:], in_=ident), writes=["idb"])
        S.dma("sp", "c1", lambda e: e.dma_start(out=id32[:], in_=ident32), writes=["id32"])
        S.op("dve", lambda e: e.memset(ones32[:], 1.0), writes=["ones32"])
        S.op("dve", lambda e: e.memset(onesb[:], 1.0), writes=["onesb"])
        S.op("dve", lambda e: e.memset(epst[:], EPS), writes=["epst"])
        mid = ExitStack()
        hT = T(mid, "hT", [128, 16, NT], BF16)
        WA = T(mid, "WA", [128, 16, 512], BF16)
        WB = T(mid, "WB", [128, 16, 512], BF16)

        evac_i = [0]

        def evac(out_ap, in_ap, reads, writes, func=None):
            evac_i[0] += 1
            if func is not None or evac_i[0] % 2 == 0:
                f = func if func is not None else AF.Copy
                S.op("act", lambda e: e.activation(out=out_ap, in_=in_ap, func=f), reads, writes)
            else:
                S.op("dve", lambda e: e.tensor_copy(out=out_ap, in_=in_ap), reads, writes)

        def load_w(buf, bufname, col0, ncols=512, queue="pool"):
            S.dma(queue, "w_" + bufname,
                  lambda e: e.dma_start(out=buf[:, :, 0:ncols], in_=w_in_v[:, :, col0:col0 + ncols]),
                  writes=[bufname])

        with ExitStack() as ph:
            cv = T(ph, "cv", [33, D], F32)
            cT = T(ph, "cT", [128, 16, 33], F32)
            bmc = [T(ph, "bmc%d" % i, [1, 512], F32) for i in range(2)]
            mrow = [T(ph, "mrow%d" % i, [33, 512], F32) for i in range(2)]
            wm = [T(ph, "wm%d" % i, [128, 8, 512], F32) for i in range(2)]
            wm = [w[:] for w in wm] + [w[:].rearrange("p k n -> p (k n)").bitcast(F32).rearrange("p (k n) -> p k n", n=512) for w in (WA, WB)]
            wmn = ["wm0", "wm1", "WA", "WB"]
            modT = T(ph, "modT", [128, 2, 16, 33], F32)
            xt = [T(ph, "xt%d" % i, [128, D], F32) for i in range(2)]
            sq = T(ph, "sq", [128, D], BF16)
            xs = [T(ph, "xs%d" % i, [128, D], BF16) for i in range(2)]
            st = [T(ph, "st%d" % i, [128, 4], F32) for i in range(2)]
            w_mod_v = w_mod.rearrange("(kc p) n -> p kc n", p=128)
            S.op("dve", lambda e: e.memset(cv[:], 0.0), writes=["cv"])
            S.dma("sp", "c0", lambda e: e.dma_start(out=cv[0:1, :], in_=cvec[0:1, :]), writes=["cv"])
            S.dma("sp", "c1", lambda e: e.dma_start(out=cv[32:33, :], in_=cvec[1:2, :]), writes=["cv"])

            def load_wm0(cb):
                for kh in range(2):
                    sl = (2 * cb + kh) % 4
                    S.dma("pool", "wmd%d" % sl,
                          lambda e, sl=sl, kh=kh: e.dma_start(out=wm[sl], in_=w_mod_v[:, kh * 8:(kh + 1) * 8, cb * 512:(cb + 1) * 512]),
                          writes=[wmn[sl]])
                S.dma("sp", "bmc%d" % (cb % 2),
                      lambda e: e.dma_start(out=bmc[cb % 2][:], in_=b_mod[0:1, cb * 512:(cb + 1) * 512]),
                      writes=["bmc%d" % (cb % 2)])
            load_wm0(0)
            load_wm0(1)
            S.op("act", lambda e: e.activation(out=cv[:], in_=cv[:], func=AF.Silu), reads=["cv"], writes=["cv"])
            for half in range(2):
                pv = pa[half][:, 0:264].rearrange("p (k m) -> p k m", m=33)

                def fn(e, half=half, pv=pv):
                    ins = None
                    for k in range(8):
                        kc = half * 8 + k
                        ins = e.transpose(pv[:, k, :], cv[0:33, kc * 128:(kc + 1) * 128], id32[0:33, 0:33])
                    return ins
                S.op("pe", fn, reads=["cv", "id32"], writes=["pa%d" % half])
                S.op("dve", lambda e, half=half, pv=pv: e.tensor_copy(out=cT[:, half * 8:half * 8 + 8, :], in_=pv),
                     reads=["pa%d" % half], writes=["cT%d" % half])

            def x_tile(t):
                s = t % 2
                src_ap = x[t * 128:(t + 1) * 128, :] if t < 16 else ctx[(t - 16) * 128:(t - 15) * 128, :]
                S.dma("sp", "xt%d" % s, lambda e: e.dma_start(out=xt[s][:], in_=src_ap), writes=["xt%d" % s])
                S.op("act", lambda e: e.activation(out=sq[:], in_=xt[s][:], func=AF.Square, accum_out=st[s][:, 0:1]),
                     reads=["xt%d" % s], writes=["sq", "st%d" % s])
                S.op("act", lambda e: e.activation(out=st[s][:, 1:2], in_=st[s][:, 0:1], func=AF.Ln, scale=1.0 / D, bias=epst[:]),
                     reads=["st%d" % s, "epst"], writes=["st%d" % s])
                S.op("act", lambda e: e.activation(out=st[s][:, 2:3], in_=st[s][:, 1:2], func=AF.Exp, scale=-0.5),
                     reads=["st%d" % s], writes=["st%d" % s])
                S.op("dve", lambda e: e.tensor_scalar(out=xs[s][:], in0=xt[s][:], scalar1=st[s][:, 2:3], scalar2=None, op0=ALU.mult),
                     reads=["xt%d" % s, "st%d" % s], writes=["xs%d" % s])
                for half in range(2):
                    pv = pt[half].rearrange("p (k m) -> p k m", m=128)

                    def fn(e, half=half, pv=pv):
                        ins = None
                        for k in range(8):
                            kc = half * 8 + k
                            ins = e.transpose(pv[:, k, :], xs[s][:, kc * 128:(kc + 1) * 128], idb[:])
                        return ins
                    S.op("pe", fn, reads=["xs%d" % s, "idb"], writes=["pt%d" % half])
                    evac(hT[:, half * 8:half * 8 + 8, t * 128:(t + 1) * 128], pv, ["pt%d" % half], ["hT"])

            nxt_tile = 0
            for cb in range(12):
                slot = cb % 2
                col = cb * 512
                pairs = []
                for kh in range(2):
                    sl = (2 * cb + kh) % 4
                    pairs += [(cT[:, kh * 8 + k, :], wm[sl][:, k, :]) for k in range(8)]
                pairs.append((ones32[0:1, 0:33], bmc[slot][0:1, :]))
                mm(S, pa[slot][0:33, :], pairs, reads=["cT0", "cT1", wmn[(2 * cb) % 4], wmn[(2 * cb + 1) % 4], "bmc%d" % slot, "ones32"],
                   writes=["pa%d" % slot])
                if cb + 2 < 12:
                    load_wm0(cb + 2)
                isscale = (D <= col < 2 * D)
                S.op("act", lambda e, slot=slot, isscale=isscale: e.activation(
                    out=mrow[slot][:], in_=pa[slot][0:33, :], func=AF.Identity, bias=(ones32[0:33, 0:1] if isscale else 0.0)),
                     reads=["pa%d" % slot, "ones32"], writes=["mrow%d" % slot])
                if col >= 2 * D:
                    S.dma("sp", "gs%d" % slot, lambda e, slot=slot, col=col: e.dma_start(out=gscr[0:1, col - 2 * D:col - 2 * D + 512], in_=mrow[slot][0:1, :]),
                          reads=["mrow%d" % slot], writes=["gscr"])
                else:
                    w_i = 0 if col < D else 1
                    kc0 = (col % D) // 128
                    pv = pb[slot][:, 0:132].rearrange("p (k m) -> p k m", m=33)

                    def fn(e, slot=slot, pv=pv):
                        ins = None
                        for k in range(4):
                            ins = e.transpose(pv[:, k, :], mrow[slot][0:33, k * 128:(k + 1) * 128], id32[0:33, 0:33])
                        return ins
                    S.op("pe", fn, reads=["mrow%d" % slot, "id32"], writes=["pb%d" % slot])
                    S.op("dve", lambda e, pv=pv, w_i=w_i, kc0=kc0: e.tensor_copy(out=modT[:, w_i, kc0:kc0 + 4, :], in_=pv),
                         reads=["pb%d" % slot], writes=["modT"])
                while nxt_tile < 18 and nxt_tile < (cb + 1) * 18 // 12:
                    x_tile(nxt_tile)
                    nxt_tile += 1
            while nxt_tile < 18:
                x_tile(nxt_tile)
                nxt_tile += 1
            for kc in range(16):
                for c0, c1, w_c in ((0, SEQ, 0), (SEQ, NT, 32)):
                    if (kc + (c0 > 0)) % 2 == 0:
                        S.op("dve", lambda e, kc=kc, c0=c0, c1=c1, w_c=w_c: e.tensor_scalar(
                            out=hT[:, kc, c0:c1], in0=hT[:, kc, c0:c1], scalar1=modT[:, 1, kc, w_c:w_c + 1], scalar2=modT[:, 0, kc, w_c:w_c + 1],
                            op0=ALU.mult, op1=ALU.add), reads=["hT", "modT"], writes=["hTm%d_%d" % (kc, c0)])
                    else:
                        S.op("act", lambda e, kc=kc, c0=c0, c1=c1, w_c=w_c: e.activation(
                            out=hT[:, kc, c0:c1], in_=hT[:, kc, c0:c1], func=AF.Identity, scale=modT[:, 1, kc, w_c:w_c + 1], bias=modT[:, 0, kc, w_c:w_c + 1]),
                            reads=["hT", "modT"], writes=["hTm%d_%d" % (kc, c0)])
            S.barrier()
        if debug:
            S.dma("sp", "dbg", lambda e: e.dma_start(out=dbg_ht, in_=hT[:].rearrange("p k n -> p (k n)")), reads=["hT"])
        if stop_after == "P1":
            S.finish("sp")
            mid.close()
            return nc

        with ExitStack() as ph:
            U = T(ph, "U", [128, 16, FW], BF16)
            NW = 257
            CS = [[T(ph, "cs%d_%d" % (i, j), [128, 16, NW], BF16) for j in range(2)] for i in range(2)]
            Y1 = T(ph, "Y1", [128, 8, 2, NW], BF16)
            ccs = T(ph, "ccs", [128, 2, 256], BF16)
            nscs = T(ph, "nscs", [128, 2, 256], BF16)
            Bsb = [T(ph, "Bsb%d" % i, [128, NW], F32) for i in range(2)]
            szf = [T(ph, "szf%d" % i, [128, 512], F32) for i in range(2)]
            tmpf = [T(ph, "tmpf%d" % i, [128, 2, 256], F32) for i in range(2)]
            fzs = [T(ph, "fzs%d" % i, [128, 2, 8, 256], BF16) for i in range(2)]
            ptZ = [p.bitcast(F32) for p in pt]
            S.dma("sp", "c0", lambda e: e.dma_start(out=ccs[:], in_=cc.rearrange("(k p) n -> p k n", p=128)), writes=["ccs"])
            S.dma("sp", "c1", lambda e: e.dma_start(out=nscs[:], in_=nsc.rearrange("(k p) n -> p k n", p=128)), writes=["nscs"])
            cn_v = cn.rearrange("(t p) n -> p t n", p=128)
            sn_v = sn.rearrange("(t p) n -> p t n", p=128)
            load_w(WA, "WA", 0)
            load_w(WB, "WB", 512)
            wbufs = [(WA, "WA"), (WB, "WB")]
            k = 0
            for cb in range(2):
                wb, wn = wbufs[cb]
                for t in range(16):
                    slot = k % 2
                    k += 1
                    mm(S, pa[slot][:, :], [(hT[:, kc, t * 128:(t + 1) * 128], wb[:, kc, :]) for kc in range(16)],
                       reads=["hT", wn], writes=["pa%d" % slot])
                    evac(U[:, t, cb * 512:(cb + 1) * 512], pa[slot][:, :], ["pa%d" % slot], ["U"])
            load_w(WA, "WA", OFF_ZF)
            load_w(WB, "WB", OFF_ZF + 512)
            fzt_v3 = fzt.rearrange("(c p) n -> p c n", p=128)
            for b in range(4):
                s = b % 2
                n0 = b * 256
                m0 = (7 - b) * 256
                S.dma("sp", "cs%d_0" % s, lambda e, s=s, n0=n0: e.dma_start(out=CS[s][0][:], in_=cn_v[:, :, n0:n0 + NW]),
                      writes=["cs%d_0" % s])
                S.dma("sp", "cs%d_1" % s, lambda e, s=s, n0=n0: e.dma_start(out=CS[s][1][:], in_=sn_v[:, :, n0:n0 + NW]),
                      writes=["cs%d_1" % s])
                for c in range(8):
                    for tr in range(2):
                        mm(S, pa[tr][:, 0:NW],
                           [(U[:, t, c * 128:(c + 1) * 128], CS[s][tr][:, t, :]) for t in range(16)],
                           reads=["U", "cs%d_%d" % (s, tr)], writes=["pa%d" % tr])
                    S.op("act", lambda e, c=c: e.activation(out=Y1[:, c, 0, :], in_=pa[0][:, 0:NW], func=AF.Copy),
                         reads=["pa0"], writes=["Y1_%d_0" % c])
                    S.op("dve", lambda e, c=c: e.tensor_copy(out=Y1[:, c, 1, :], in_=pa[1][:, 0:NW]),
                         reads=["pa1"], writes=["Y1_%d_1" % c])
                for g in range(4):
                    for cp in range(2):
                        c = 2 * g + cp
                        k = c % 2
                        pA, nA = (pb[0], "pb0A") if k == 0 else (pb[2], "pb2A")
                        pB, nB = (pb[1], "pb1B") if k == 0 else (pb[3], "pb3B")
                        pZ, nZ = ptZ[k], "pt%dZ" % k
                        mm(S, pA[:, 0:NW], [(ccs[:, ci, cp * 128:(cp + 1) * 128], Y1[:, 2 * g + ci, 0, :]) for ci in range(2)],
                           reads=["ccs", "Y1_%d_0" % (2 * g), "Y1_%d_0" % (2 * g + 1)], writes=[nA])
                        mm(S, pB[:, 0:NW], [(nscs[:, ci, cp * 128:(cp + 1) * 128], Y1[:, 2 * g + ci, 1, :]) for ci in range(2)],
                           reads=["nscs", "Y1_%d_1" % (2 * g), "Y1_%d_1" % (2 * g + 1)], writes=[nB])
                        wb, wn = wbufs[c // 4]
                        wsl = slice((c % 4) * 128, (c % 4 + 1) * 128)
                        mm(S, pZ[:, 0:256], [(wb[:, kc, wsl], hT[:, kc, n0:n0 + 256]) for kc in range(16)], reads=["hT", wn], writes=[nZ])
                        mm(S, pZ[:, 256:512], [(wb[:, kc, wsl], hT[:, kc, m0:m0 + 256]) for kc in range(16)], reads=["hT", wn], writes=[nZ])
                        S.op("act", lambda e, k=k, pB=pB: e.activation(out=Bsb[k][:], in_=pB[:, 0:NW], func=AF.Copy),
                             reads=[nB], writes=["Bsb%d" % k])
                        S.op("act", lambda e, k=k, pZ=pZ: e.activation(out=szf[k][:], in_=pZ[:, 0:512], func=AF.Silu),
                             reads=[nZ], writes=["szf%d" % k])
                        S.op("dve", lambda e, k=k, pA=pA: e.tensor_tensor(out=tmpf[k][:, 0, :], in0=pA[:, 0:256], in1=Bsb[k][:, 0:256], op=ALU.add),
                             reads=[nA, "Bsb%d" % k], writes=["tmpf%d_0" % k])
                        S.op("dve", lambda e, k=k, pA=pA: e.tensor_tensor(out=tmpf[k][:, 1, :], in0=pA[:, 256:0:-1], in1=Bsb[k][:, 256:0:-1], op=ALU.subtract),
                             reads=[nA, "Bsb%d" % k], writes=["tmpf%d_1" % k])
                        S.op("dve", lambda e, k=k, s=s, c=c: e.tensor_tensor(out=fzs[s][:, :, c, :], in0=tmpf[k][:], in1=szf[k][:].rearrange("p (a n) -> p a n", a=2), op=ALU.mult),
                             reads=["tmpf%d_0" % k, "tmpf%d_1" % k, "szf%d" % k], writes=["fzs%d" % s])
                S.dma("sp", "fzo%d" % s, lambda e, s=s, n0=n0: e.dma_start(out=fzt_v3[:, :, n0:n0 + 256], in_=fzs[s][:, 0, :, :]),
                      reads=["fzs%d" % s], writes=["fzt"])
                S.dma("sp", "fzm%d" % s, lambda e, s=s, m0=m0: e.dma_start(out=fzt_v3[:, :, m0:m0 + 256], in_=fzs[s][:, 1, :, :]),
                      reads=["fzs%d" % s], writes=["fzt"])
            S.barrier()
        if stop_after == "PF":
            S.finish("sp")
            mid.close()
            return nc

        with ExitStack() as ph:
            tabs = {n: T(ph, "tb_" + n, [128, 18, 64], BF16) for n in
                    ("Aq", "Bq", "Cq", "Dq", "Ak", "Bk", "Ck", "Dk")}
            load_w(WA, "WA", OFF_Q)
            load_w(WB, "WB", OFF_K)
            with ExitStack() as ph2:
                rc = T(ph2, "rc", [128, 18, 64], F32)
                rsn = T(ph2, "rsn", [128, 18, 64], F32)
                gbc = T(ph2, "gbc", [128, 256], F32)
                S.dma("sp", "c0", lambda e: e.dma_start(out=rc[:], in_=ropec.rearrange("(t p) f -> p t f", p=128)), writes=["rc"])
                S.dma("sp", "c1", lambda e: e.dma_start(out=rsn[:], in_=ropes.rearrange("(t p) f -> p t f", p=128)), writes=["rsn"])
                S.dma("sp", "c2", lambda e: e.dma_start(out=gbc[:], in_=qk_gain.partition_broadcast(128)[:, 0, :]), writes=["gbc"])
                scale_q = float(DH) ** -0.5
                S.op("dve", lambda e: e.tensor_scalar(out=gbc[:, 0:128], in0=gbc[:, 0:128], scalar1=scale_q, scalar2=None, op0=ALU.mult),
                     reads=["gbc"], writes=["gbc"])
                for pre, goff in (("q", 0), ("k", 128)):
                    gv = gbc[:, goff:goff + 128].rearrange("p (a h f) -> p a h f", a=2, h=2)
                    g0 = gv[:, :, 0, :]
                    g1 = gv[:, :, 1, :]
                    for name, g, tab, tabn in (("A", g0, rc, "rc"), ("B", g1, rsn, "rsn"), ("C", g1, rc, "rc"), ("D", g0, rsn, "rsn")):
                        dst = tabs[name + pre]

                        def fn(e, dst=dst, g=g, tab=tab):
                            return e.tensor_tensor(
                                out=dst[:].rearrange("p t (a f) -> p t a f", a=2),
                                in0=tab[:].rearrange("p t (a f) -> p t a f", a=2),
                                in1=g.unsqueeze(1).broadcast_to([128, 18, 2, 32]), op=ALU.mult)
                        S.op("dve", fn, reads=[tabn, "gbc"], writes=["tb_" + name + pre])
                S.barrier()
            QT = T(ph, "QT", [128, 4, SEQ], BF16)
            KT = T(ph, "KT", [128, 4, NT], BF16)
            V = T(ph, "V", [128, 18, 512], BF16)
            scr = [T(ph, "scr0", [128, 512], BF16)]
            qn = [T(ph, "qn%d" % i, [128, 512], F32) for i in range(2)]
            qr = [T(ph, "qr%d" % i, [128, 512], BF16) for i in range(2)]
            rt = [T(ph, "rt%d" % i, [128, 4, 256], F32) for i in range(2)]
            ss = [T(ph, "ss%d" % i, [128, 12], F32) for i in range(2)]
            TB = [T(ph, "TB%d" % i, [128, 2 * NI * 64], BF16) for i in range(2)]
            exb = [T(ph, "exb%d" % i, [128, 512], BF16) for i in range(5)]
            ptS = [p.bitcast(F32) for p in pt]
            sza = [T(ph, "sza0", [128, 256], F32)]
            rden = [T(ph, "rden0", [128, 256], F32)]
            o1 = [T(ph, "o1_0", [128, 256], F32)]
            ozs = [T(ph, "ozs%d" % i, [128, 256], BF16) for i in range(2)]

            tile_i = [0]

            def qk_tile(ps, psn, pre, t, dstT):
                s = tile_i[0] % 2
                tile_i[0] += 1
                S.op("act", lambda e: e.activation(out=scr[0][:], in_=ps, func=AF.Square), reads=[psn], writes=["scr0"])
                S.op("dve", lambda e: e.tensor_reduce(out=ss[s][:, 0:4], in_=scr[0][:].rearrange("p (h d) -> p h d", h=4), axis=AX.X, op=ALU.add),
                     reads=["scr0"], writes=["ss%d" % s])
                S.op("act", lambda e: e.activation(out=ss[s][:, 4:8], in_=ss[s][:, 0:4], func=AF.Ln, scale=1.0 / DH, bias=epst[:]),
                     reads=["ss%d" % s, "epst"], writes=["ss%d" % s])
                S.op("act", lambda e: e.activation(out=ss[s][:, 8:12], in_=ss[s][:, 4:8], func=AF.Exp, scale=-0.5),
                     reads=["ss%d" % s], writes=["ss%d" % s])
                S.op("dve", lambda e: e.tensor_tensor(out=qn[s][:].rearrange("p (h d) -> p h d", h=4), in0=ps.rearrange("p (h d) -> p h d", h=4),
                                                      in1=ss[s][:, 8:12].unsqueeze(2).broadcast_to([128, 4, 128]), op=ALU.mult),
                     reads=[psn, "ss%d" % s], writes=["qn%d" % s])
                qv = qn[s][:].rearrange("p (h a b f) -> p h a b f", h=4, a=2, b=2)
                ov = qr[s][:].rearrange("p (h a b f) -> p h a b f", h=4, a=2, b=2)
                x0 = qv[:, :, :, 0, :]
                x1 = qv[:, :, :, 1, :]
                tb = lambda n: tabs[n + pre][:, t, :].rearrange("p (a f) -> p a f", a=2).unsqueeze(1).broadcast_to([128, 4, 2, 32])
                rv = [rt[s][:, i, :].rearrange("p (h a f) -> p h a f", h=4, a=2) for i in range(4)]
                tn = ["tb_" + n + pre for n in "ABCD"]
                qnn = "qn%d" % s
                rn = ["rt%d_%d" % (s, i) for i in range(4)]
                S.op("dve", lambda e: e.tensor_tensor(out=rv[0], in0=x0, in1=tb("A"), op=ALU.mult), reads=[qnn, tn[0]], writes=[rn[0]])
                S.op("dve", lambda e: e.tensor_tensor(out=rv[1], in0=x1, in1=tb("B"), op=ALU.mult), reads=[qnn, tn[1]], writes=[rn[1]])
                S.op("dve", lambda e: e.tensor_tensor(out=ov[:, :, :, 0, :], in0=rv[0], in1=rv[1], op=ALU.subtract),
                     reads=[rn[0], rn[1]], writes=["qr%d_a" % s])
                S.op("pool", lambda e: e.tensor_tensor(out=rv[2], in0=x1, in1=tb("C"), op=ALU.mult), reads=[qnn, tn[2]], writes=[rn[2]])
                S.op("pool", lambda e: e.tensor_tensor(out=rv[3], in0=x0, in1=tb("D"), op=ALU.mult), reads=[qnn, tn[3]], writes=[rn[3]])
                S.op("pool", lambda e: e.tensor_tensor(out=ov[:, :, :, 1, :], in0=rv[2], in1=rv[3], op=ALU.add),
                     reads=[rn[2], rn[3]], writes=["qr%d_b" % s])
                pv = pt[s][:, 0:512].rearrange("p (k m) -> p k m", m=128)

                def fin():
                    def fn(e):
                        ins = None
                        for hh in range(4):
                            ins = e.transpose(pv[:, hh, :], qr[s][:, hh * 128:(hh + 1) * 128], idb[:])
                        return ins
                    S.op("pe", fn, reads=["qr%d_a" % s, "qr%d_b" % s, "idb"], writes=["pt%d" % s])
                    evac(dstT[:, :, t * 128:(t + 1) * 128], pv, ["pt%d" % s], [pre + "T"])
                return fin

            def load_tb(h, hs):
                hw = NI * 64
                for tabi in range(2):
                    stg = rt[tabi][:].rearrange("p a b -> p (a b)")[:, 0:hw]
                    rn = ["rt%d_%d" % (tabi, i) for i in range(4)]
                    S.dma("sp", "tbs%d" % tabi, lambda e, stg=stg, tabi=tabi: e.dma_start(out=stg, in_=btab[h][:, tabi * hw:(tabi + 1) * hw]),
                          writes=rn)
                    S.op("act", lambda e, stg=stg, tabi=tabi: e.activation(out=TB[hs][:, tabi * hw:(tabi + 1) * hw], in_=stg, func=AF.Exp),
                         reads=rn, writes=["TB%d" % hs])

            def attention_head(hl, h, hs):
                def prologue():
                    if h + 1 < H:
                        load_tb(h + 1, (hs + 1) % 2)
                tbv = TB[hs][:].rearrange("p (t i c) -> p t i c", t=2, i=NI)
                units = []
                for a in range(8):
                    chunks = [("loc", j) for j in ATT_BLOCKS[a]] + [("ctx", 16), ("ctx", 17)]
                    for ci, (kind, j) in enumerate(chunks):
                        units.append((a, ci, len(chunks), kind, j))
                npair = len(units) // 2
                banks = (pa[0][:], pa[1][:], pb[0][:], ptS[0], ptS[1])
                bnames = ("pa0S", "pa1S", "pb0S", "pt0S", "pt1S")

                def front(pi):
                    sb = pi % 5
                    bank, bname, en = banks[sb], bnames[sb], "exb%d" % sb

                    def fn(e):
                        ins = None
                        for k in range(2):
                            a, ci, nch, kind, j = units[2 * pi + k]
                            ins = e.matmul(bank[:, k * 256:(k + 1) * 256], KT[:, hl, j * 128:(j + 1) * 128],
                                           QT[:, hl, a * 256:(a + 1) * 256], start=True, stop=True)
                        return ins
                    S.op("pe", fn, reads=["kT", "qT"], writes=[bname])
                    S.op("act", lambda e: e.activation(out=exb[sb][:], in_=bank[:, 0:512], func=AF.Exp), reads=[bname], writes=[en])
                    for k in range(2):
                        a, ci, nch, kind, j = units[2 * pi + k]
                        if kind == "loc":
                            tab = 1 if a in (0, 7) else 0
                            i0 = 4 * a - 2 * j + IOFF
                            ev = exb[sb][:, k * 256:(k + 1) * 256].rearrange("p (i c) -> p i c", i=4)
                            S.op("dve", lambda e, ev=ev, tab=tab, i0=i0: e.tensor_tensor(out=ev, in0=ev, in1=tbv[:, tab, i0:i0 + 4, :], op=ALU.mult),
                                 reads=[en, "TB%d" % hs], writes=[en])

                def back(pi):
                    sb = pi % 5
                    en = "exb%d" % sb
                    poO = pb[1][:, 0:256]
                    poD = pb[2][:, 0:256]

                    def fn(e):
                        ins = None
                        for k in range(2):
                            a, ci, nch, kind, j = units[2 * pi + k]
                            pm = exb[sb][:, k * 256:(k + 1) * 256]
                            e.matmul(poO, V[:, j, hl * 128:(hl + 1) * 128], pm, start=(ci == 0), stop=(ci == nch - 1))
                            ins = e.matmul(poD, onesb[:], pm, start=(ci == 0), stop=(ci == nch - 1))
                        return ins
                    S.op("pe", fn, reads=["V", en, "onesb"], writes=["pb1O", "pb2D"])
                    a, ci, nch, kind, j = units[2 * pi + 1]
                    if ci == nch - 1:
                        bs = a % 2
                        mm(S, pb[3][:, 0:256], [(WB[:, kc, hl * 128:(hl + 1) * 128], hT[:, kc, a * 256:(a + 1) * 256]) for kc in range(16)],
                           reads=["hT", "WB"], writes=["pb3Z"])
                        S.op("act", lambda e: e.activation(out=rden[0][:], in_=poD, func=AF.Copy),
                             reads=["pb2D"], writes=["rden0"])
                        S.op("act", lambda e: e.activation(out=o1[0][:], in_=poO, func=AF.Copy, scale=0.5),
                             reads=["pb1O"], writes=["o1_0"])
                        S.op("act", lambda e: e.activation(out=sza[0][:], in_=pb[3][:, 0:256], func=AF.Tanh, scale=0.5),
                             reads=["pb3Z"], writes=["sza0"])
                        S.op("dve", lambda e: e.reciprocal(out=rden[0][:], in_=rden[0][:]),
                             reads=["rden0"], writes=["rden0"])
                        S.op("dve", lambda e: e.scalar_tensor_tensor(out=sza[0][:], in0=sza[0][:], scalar=1.0, in1=pb[3][:, 0:256], op0=ALU.add, op1=ALU.mult),
                             reads=["sza0", "pb3Z"], writes=["sza0"])
                        S.op("pool", lambda e: e.tensor_tensor(out=o1[0][:], in0=o1[0][:], in1=rden[0][:], op=ALU.mult),
                             reads=["o1_0", "rden0"], writes=["o1_0"])
                        S.op("pool", lambda e: e.tensor_tensor(out=ozs[bs][:], in0=o1[0][:], in1=sza[0][:], op=ALU.mult),
                             reads=["o1_0", "sza0"], writes=["ozs%d" % bs])
                        S.dma("sp", "ozo%d" % bs, lambda e: e.dma_start(out=ozt[h * 128:(h + 1) * 128, a * 256:(a + 1) * 256], in_=ozs[bs][:]),
                              reads=["ozs%d" % bs], writes=["ozt"])
                fr = [(lambda pi=pi: front(pi)) for pi in range(npair)]
                fr[0] = (lambda: (prologue(), front(0)))
                bk = [(lambda pi=pi: back(pi)) for pi in range(npair)]
                return fr, bk

            hcount = 0
            load_tb(0, 0)
            for g in range(4):
                if g > 0:
                    load_w(WB, "WB", OFF_K + g * 512)
                k = 0
                prev = None
                for pre, wbuf, wn, dst, ntile in (("q", WA, "WA", QT, 16), ("k", WB, "WB", KT, 18)):
                    for t in range(ntile):
                        slot = k % 2
                        k += 1
                        mm(S, pa[slot][:, :], [(hT[:, kc, t * 128:(t + 1) * 128], wbuf[:, kc, :]) for kc in range(16)],
                           reads=["hT", wn], writes=["pa%d" % slot])
                        fin = qk_tile(pa[slot][:, :], "pa%d" % slot, pre, t, dst)
                        if not _DEFER:
                            fin()
                            continue
                        if prev is not None:
                            prev()
                        prev = fin
                    if pre == "q":
                        load_w(WA, "WA", OFF_V + g * 512)
                if prev is not None:
                    prev()
                load_w(WB, "WB", OFF_ZA + g * 512)
                for t in range(18):
                    slot = k % 2
                    k += 1
                    mm(S, pa[slot][:, :], [(hT[:, kc, t * 128:(t + 1) * 128], WA[:, kc, :]) for kc in range(16)],
                       reads=["hT", "WA"], writes=["pa%d" % slot])
                    evac(V[:, t, :], pa[slot][:, :], ["pa%d" % slot], ["V"])
                if g + 1 < 4:
                    load_w(WA, "WA", OFF_Q + (g + 1) * 512)
                fronts, backs = [], []
                for hl in range(4):
                    fr, bk = attention_head(hl, g * 4 + hl, hcount % 2)
                    fronts += fr
                    backs += bk
                    hcount += 1
                for n in range(len(fronts) + _LA):
                    if n < len(fronts):
                        fronts[n]()
                    if n >= _LA:
                        backs[n - _LA]()
            if debug:
                dq = nc.dram_tensor("dbg_qt", [128, 4 * SEQ], BF16, kind="ExternalOutput").ap()
                dk = nc.dram_tensor("dbg_kt", [128, 4 * NT], BF16, kind="ExternalOutput").ap()
                dv = nc.dram_tensor("dbg_v", [128, 18 * 512], BF16, kind="ExternalOutput").ap()
                S.dma("sp", "dbg", lambda e: e.dma_start(out=dq, in_=QT[:].rearrange("p k n -> p (k n)")), reads=["qT"])
                S.dma("sp", "dbg", lambda e: e.dma_start(out=dk, in_=KT[:].rearrange("p k n -> p (k n)")), reads=["kT"])
                S.dma("sp", "dbg", lambda e: e.dma_start(out=dv, in_=V[:].rearrange("p k n -> p (k n)")), reads=["V"])
            S.barrier()
        if stop_after == "PA":
            S.finish("sp")
            mid.close()
            return nc

        with ExitStack() as ph:
            OZ = T(ph, "OZ", [128, 16, SEQ], BF16)
            FZb = [T(ph, "FZb%d" % i, [128, 8, 512], BF16) for i in range(2)]
            WM = [w[:].rearrange("p k n -> p (k n)")[:, 0:7168].rearrange("p (k n) -> p k n", n=128) for w in (WA, WB)]
            sg = [T(ph, "sg%d" % i, [128, 2, 512], F32) for i in range(2)]
            ty = [T(ph, "ty%d" % i, [128, 2, 512], F32) for i in range(2)]
            ys = [T(ph, "ys%d" % i, [128, 512], BF16) for i in range(2)]
            wf_v = w_f_out.rearrange("(kc p) n -> p kc n", p=128)
            wa_v = w_a_out.rearrange("(kc p) n -> p kc n", p=128)
            ozt_v = ozt.rearrange("(c p) n -> p c n", p=128)
            fzt_v = fzt.rearrange("(c p) n -> p c n", p=128)

            def load_wm(slot, i):
                c0 = i * 128
                S.dma("pool", "wm%da" % slot, lambda e: e.dma_start(out=WM[slot][:, 0:8, :], in_=wf_v[:, :, c0:c0 + 128]), writes=["WM%d" % slot])
                S.dma("pool", "wm%db" % slot, lambda e: e.dma_start(out=WM[slot][:, 8:24, :], in_=wa_v[:, :, c0:c0 + 128]), writes=["WM%d" % slot])
                S.dma("pool", "wm%dc" % slot, lambda e: e.dma_start(out=WM[slot][:, 24:40, :], in_=w_in_v[:, :, OFF_GF + c0:OFF_GF + c0 + 128]), writes=["WM%d" % slot])
                S.dma("pool", "wm%dd" % slot, lambda e: e.dma_start(out=WM[slot][:, 40:56, :], in_=w_in_v[:, :, OFF_GA + c0:OFF_GA + c0 + 128]), writes=["WM%d" % slot])

            cnt = 0
            pairs3 = [(pb[0], pb[1], "pb0", "pb1"), (pb[2], pb[3], "pb2", "pb3"), (pa[0], pa[1], "pa0", "pa1")]
            item = [0]
            for tb in range(4):
                S.dma("sp", "ozl%d" % tb, lambda e, tb=tb: e.dma_start(out=OZ[:, :, tb * 512:(tb + 1) * 512], in_=ozt_v[:, :, tb * 512:(tb + 1) * 512]),
                      reads=["ozt"], writes=["OZ%d" % tb])
            load_wm(0, 0)
            if True:
                for i in range(16):
                    ws = cnt % 2
                    cnt += 1
                    if i + 1 < 16:
                        load_wm(cnt % 2, i + 1)
                    W = WM[ws]
                    wn = "WM%d" % ws
                    for tb in range(4):
                        s = (i * 4 + tb) % 2
                        n0 = tb * 512
                        t0 = 0
                        tok = slice(n0, n0 + 512)
                        if i == 0 and tb == 0:
                            S.dma("sp", "fzl0", lambda e: e.dma_start(out=FZb[0][:], in_=fzt_v[:, :, 0:512]), reads=["fzt"], writes=["FZb0"])
                        nxt = i * 4 + tb + 1
                        if nxt < 64:
                            S.dma("sp", "fzl%d" % (nxt % 2), lambda e, nxt=nxt: e.dma_start(out=FZb[nxt % 2][:], in_=fzt_v[:, :, (nxt % 4) * 512:(nxt % 4 + 1) * 512]),
                                  reads=["fzt"], writes=["FZb%d" % (nxt % 2)])
                        for br in range(2):
                            pY, pG, nY, nG = pairs3[item[0] % 3]
                            item[0] += 1
                            if br == 0:
                                mm(S, pG[:, :], [(W[:, 24 + kc, :], hT[:, kc, tok]) for kc in range(16)], reads=[wn, "hT"], writes=[nG])
                                mm(S, pY[:, :], [(W[:, kc, :], FZb[s][:, kc, :]) for kc in range(8)], reads=[wn, "FZb%d" % s], writes=[nY])
                            else:
                                mm(S, pG[:, :], [(W[:, 40 + kc, :], hT[:, kc, tok]) for kc in range(16)], reads=[wn, "hT"], writes=[nG])
                                mm(S, pY[:, :], [(W[:, 8 + kc, :], OZ[:, kc, n0:n0 + 512]) for kc in range(16)], reads=[wn, "OZ%d" % tb], writes=[nY])
                            S.op("act", lambda e, s=s, pG=pG, br=br: e.activation(out=sg[s][:, br, :], in_=pG[:, :], func=AF.Sigmoid),
                                 reads=[nG], writes=["sg%d_%d" % (s, br)])
                            S.op("dve", lambda e, s=s, pY=pY, br=br: e.tensor_tensor(out=ty[s][:, br, :], in0=pY[:, :], in1=sg[s][:, br, :], op=ALU.mult),
                                 reads=[nY, "sg%d_%d" % (s, br)], writes=["ty%d_%d" % (s, br)])
                        S.op("dve", lambda e, s=s: e.tensor_tensor(out=ys[s][:], in0=ty[s][:, 0, :], in1=ty[s][:, 1, :], op=ALU.add),
                             reads=["ty%d_0" % s, "ty%d_1" % s], writes=["ys%d" % s])
                        S.dma("sp", "yo%d" % s, lambda e, s=s, i=i, tok=tok: e.dma_start(out=yt[i * 128:(i + 1) * 128, tok], in_=ys[s][:]),
                              reads=["ys%d" % s], writes=["yt"])
            S.barrier()
        if stop_after == "M1":
            S.finish("sp")
            mid.close()
            return nc
        mid.close()

        with ExitStack() as ph:
            WOb = [T(ph, "WOb%d" % i, [128, 16, 512], BF16) for i in range(2)]
            gate = T(ph, "gate", [128, D], F32)
            yb = [T(ph, "yb%d" % i, [128, 16, 512], BF16) for i in range(2)]
            xo = [T(ph, "xo%d" % i, [128, 512], F32) for i in range(4)]
            ot = [T(ph, "ot%d" % i, [128, 512], F32) for i in range(3)]
            tg = [T(ph, "tg%d" % i, [128, 512], F32) for i in range(2)]
            wo_v = w_out.rearrange("(kc p) n -> p kc n", p=128)
            yt_v = yt.rearrange("(c p) n -> p c n", p=128)
            banks = [(pa[0], "pa0"), (pa[1], "pa1"), (pb[0], "pb0"), (pb[1], "pb1")]

            def load_wo(cb):
                S.dma("pool", "wo%d" % (cb % 2), lambda e: e.dma_start(out=WOb[cb % 2][:], in_=wo_v[:, :, cb * 512:(cb + 1) * 512]),
                      writes=["WOb%d" % (cb % 2)])

            def load_yb(j):
                tb = j % 4
                S.dma("sp", "ybl%d" % (j % 2), lambda e: e.dma_start(out=yb[j % 2][:], in_=yt_v[:, :, tb * 512:(tb + 1) * 512]),
                      reads=["yt"], writes=["yb%d" % (j % 2)])

            def load_x(k):
                cb, t = k // 16, k % 16
                S.dma("sp", "xl%d" % (k % 4), lambda e: e.dma_start(out=xo[k % 4][:], in_=x[t * 128:(t + 1) * 128, cb * 512:(cb + 1) * 512]),
                      writes=["xo%d" % (k % 4)])
            load_wo(0)
            load_wo(1)
            S.dma("sp", "gl", lambda e: e.dma_start(out=gate[:], in_=gscr.partition_broadcast(128)[:, 0, :]), reads=["gscr"], writes=["gate"])
            load_yb(0)
            load_x(0)
            load_x(1)
            for k in range(64):
                cb, t = k // 16, k % 16
                tb, tt = t // 4, t % 4
                j = cb * 4 + tb
                if tt == 0 and j + 1 < 16:
                    load_yb(j + 1)
                if k + 2 < 64:
                    load_x(k + 2)
                bank, bn = banks[k % 4]
                mm(S, bank[:, :], [(yb[j % 2][:, kc, tt * 128:(tt + 1) * 128], WOb[cb % 2][:, kc, :]) for kc in range(16)],
                   reads=["yb%d" % (j % 2), "WOb%d" % (cb % 2)], writes=[bn])
                if t == 15 and cb + 2 < 4:
                    load_wo(cb + 2)
                S.op("dve", lambda e, bank=bank, k=k, cb=cb: e.tensor_tensor(out=tg[k % 2][:], in0=bank[:, :], in1=gate[:, cb * 512:(cb + 1) * 512], op=ALU.mult),
                     reads=[bn, "gate"], writes=["tg%d" % (k % 2)])
                S.op("dve", lambda e, k=k: e.tensor_tensor(out=ot[k % 3][:], in0=tg[k % 2][:], in1=xo[k % 4][:], op=ALU.add),
                     reads=["tg%d" % (k % 2), "xo%d" % (k % 4)], writes=["ot%d" % (k % 3)])
                S.dma("act", "oo%d" % (k % 3), lambda e, k=k, cb=cb, t=t: e.dma_start(out=out[t * 128:(t + 1) * 128, cb * 512:(cb + 1) * 512], in_=ot[k % 3][:]),
                      reads=["ot%d" % (k % 3)], writes=["out"])
            S.finish("sp")
    return nc


_CONSTS = None


def make_in_maps(x, c, ctx, c_ctx, w_mod, b_mod, w_in, q_gain, k_gain, rpb, w_f_out, w_a_out, w_out, cores=range(8)):
    global _CONSTS
    if _CONSTS is None:
        _CONSTS = host_consts()
    f = lambda a: np.ascontiguousarray(np.asarray(a, dtype=np.float32))
    x, c, ctx, c_ctx = f(x), f(c), f(ctx), f(c_ctx)
    shared = dict(
        w_mod=f(w_mod)[0], b_mod=f(b_mod)[0][None, :], w_in=f(w_in)[0],
        qk_gain=np.concatenate([f(q_gain)[0], f(k_gain)[0]])[None, :],
        btab=host_bias_tables(f(rpb)[0]).reshape(H, 128, 2 * NI * 64),
        w_f_out=f(w_f_out)[0], w_a_out=f(w_a_out)[0], w_out=f(w_out)[0], **_CONSTS)
    maps = []
    for b in cores:
        m = dict(shared)
        m["x"] = x[b]
        m["ctx"] = ctx[b]
        m["cvec"] = np.stack([c[b], c_ctx], 0)
        maps.append(m)
    return maps


def kernel(x, c, ctx, c_ctx, w_mod, b_mod, w_in, q_gain, k_gain, rpb, w_f_out, w_a_out, w_out):
    maps = make_in_maps(x, c, ctx, c_ctx, w_mod, b_mod, w_in, q_gain, k_gain, rpb, w_f_out, w_a_out, w_out)
    nc = build_nc()
    res = run_bass_kernel_spmd(nc, maps, core_ids=list(range(8)))
    return np.stack([np.asarray(r["out"], dtype=np.float32) for r in res.results], 0)
```
